# Optimizing a Trainium2 kernel written in Bass

```python
import jax, jax.numpy as jnp
from jax import lax
import numpy as np

D_MODEL = 1024
BATCH = 16
SEQ = 2048
DEPTH = 1

CHUNK = 64
Q_BLOCK = 128
ROPE_THETA = 10000.0
EPS = 1e-6
GN_EPS = 1e-5
NEG_BIG = -1e30
A_HEADS = 8
A_HEAD_DIM = 64
IDX_HEADS = 8
IDX_DIM = 64
TOPK_MAX = 256
A_WIDTH = A_HEADS * A_HEAD_DIM
R_HEADS = 4
R_QK_DIM = 128
R_V_DIM = 256
R_WIDTH = R_HEADS * R_V_DIM
D_FF = 2816
PLE_DIM = 256
IN_SPLITS = (
    A_HEADS * A_HEAD_DIM,
    A_HEAD_DIM,
    A_HEAD_DIM,
    IDX_HEADS * IDX_DIM,
    IDX_DIM,
    IDX_HEADS,
    R_HEADS * R_QK_DIM,
    R_HEADS * R_QK_DIM,
    R_HEADS * R_V_DIM,
    R_HEADS * R_V_DIM,
    D_MODEL,
    D_MODEL,
)
IN_WIDTH = sum(IN_SPLITS)

kernel_name = 'hybrid_dsa_retention_macaron'


def rms_norm(x, g):
    xf = x.astype(jnp.float32)
    y = xf * lax.rsqrt(jnp.mean(xf * xf, axis=-1, keepdims=True) + EPS)
    return (y * g.astype(jnp.float32)).astype(x.dtype)


def swiglu(x, w_gate, w_up, w_down):
    return (jax.nn.silu(x @ w_gate) * (x @ w_up)) @ w_down


def rope(t, pos):
    half = t.shape[-1] // 2
    inv = ROPE_THETA ** (-jnp.arange(half, dtype=jnp.float32) / half)
    ang = pos.astype(jnp.float32)[..., None] * inv
    cos = jnp.cos(ang)[:, :, None, :]
    sin = jnp.sin(ang)[:, :, None, :]
    t1 = t[..., :half].astype(jnp.float32)
    t2 = t[..., half:].astype(jnp.float32)
    return jnp.concatenate([t1 * cos - t2 * sin, t2 * cos + t1 * sin], axis=-1).astype(t.dtype)


def dsa_attention(q, k, v, q_idx, k_idx, w_idx, n_sel):
    b, s = q.shape[0], q.shape[1]
    nb = s // Q_BLOCK
    key_pos = jnp.arange(s)
    k_idx32 = k_idx.astype(jnp.float32)

    def to_blocks(t):
        return jnp.moveaxis(t.reshape((b, nb, Q_BLOCK) + t.shape[2:]), 1, 0)

    def one_block(args):
        qb, qib, wb, blk = args
        qpos = blk * Q_BLOCK + jnp.arange(Q_BLOCK)
        visible_end = (qpos // CHUNK + 1) * CHUNK
        admissible = key_pos[None, :] < visible_end[:, None]
        rel = jax.nn.relu(jnp.einsum('bqhd,bsd->bqhs', qib.astype(jnp.float32), k_idx32))
        score = jnp.einsum('bqhs,bqh->bqs', rel, wb.astype(jnp.float32))
        score = jnp.where(admissible[None], score, NEG_BIG)
        top_val, top_pos = lax.top_k(score, n_sel)
        valid = top_val > 0.5 * NEG_BIG
        k_sel = jax.vmap(lambda kb, ib: kb[ib])(k, top_pos)
        v_sel = jax.vmap(lambda vb, ib: vb[ib])(v, top_pos)
        logits = jnp.einsum('bqhd,bqkd->bqhk', qb, k_sel).astype(jnp.float32) * (A_HEAD_DIM ** -0.5)
        logits = jnp.where(valid[:, :, None, :], logits, NEG_BIG)
        probs = jax.nn.softmax(logits, axis=-1).astype(v.dtype)
        return jnp.einsum('bqhk,bqkd->bqhd', probs, v_sel)

    out = lax.map(one_block, (to_blocks(q), to_blocks(q_idx), to_blocks(w_idx), jnp.arange(nb)))
    return jnp.moveaxis(out, 0, 1).reshape(b, s, -1)


def retention(q, k, v):
    b, s, h, dk = q.shape
    dv = v.shape[-1]
    n = s // CHUNK
    log_gamma = jnp.log1p(-jnp.exp2(-5.0 - jnp.arange(h, dtype=jnp.float32)))
    idx = jnp.arange(CHUNK, dtype=jnp.float32)
    decay_intra = jnp.exp(log_gamma[:, None, None] * jnp.abs(idx[:, None] - idx[None, :]))
    decay_q = jnp.exp(log_gamma[None, :] * (idx[:, None] + 1.0))
    decay_k = jnp.exp(log_gamma[None, :] * (CHUNK - 1.0 - idx[:, None]))
    decay_chunk = jnp.exp(log_gamma * CHUNK)
    qc = q.reshape(b, n, CHUNK, h, dk)
    kc = (k * (dk ** -0.5)).reshape(b, n, CHUNK, h, dk)
    vc = v.reshape(b, n, CHUNK, h, dv)
    scores = jnp.einsum('bnihd,bnjhd->bnhij', qc, kc) * decay_intra
    y_intra = jnp.einsum('bnhij,bnjhe->bnihe', scores, vc)
    kv = jnp.einsum('bnjhd,bnjhe,jh->nbhde', kc, vc, decay_k)

    def step(state, kv_n):
        return decay_chunk[None, :, None, None] * state + kv_n, state

    _, prev = lax.scan(step, jnp.zeros((b, h, dk, dv), jnp.float32), kv)
    y_cross = jnp.einsum('bnihd,nbhde->bnihe', qc, prev) * decay_q[None, None, :, :, None]
    return (y_intra + y_cross).reshape(b, s, h, dv)


def group_norm_heads(y, g):
    mean = jnp.mean(y, axis=-1, keepdims=True)
    var = jnp.mean(jnp.square(y - mean), axis=-1, keepdims=True)
    yn = (y - mean) * lax.rsqrt(var + GN_EPS)
    return yn.reshape(y.shape[0], y.shape[1], -1) * g.astype(jnp.float32)


def setup_inputs(seed: int = 0) -> dict:
    key = jax.random.key(seed)
    ks = jax.random.split(key, 24)
    f32 = jnp.float32

    def w(k, shape, fan_in, scale=1.0):
        return jax.random.normal(k, shape, f32) * (scale * fan_in ** -0.5)

    def gain(k, shape):
        return 1.0 + 0.05 * jax.random.normal(k, shape, f32)

    offsets = jax.random.randint(ks[2], (BATCH, 1), 0, 4096)
    positions = (offsets + jnp.arange(SEQ)[None, :]).astype(jnp.int32)
    return {
        'x': jax.random.normal(ks[0], (BATCH, SEQ, D_MODEL), f32),
        'p': jax.random.normal(ks[1], (DEPTH, BATCH, SEQ, PLE_DIM), f32),
        'positions': positions,
        'ffn1_norm': gain(ks[3], (DEPTH, D_MODEL)),
        'ffn1_w_gate': w(ks[4], (DEPTH, D_MODEL, D_FF), D_MODEL),
        'ffn1_w_up': w(ks[5], (DEPTH, D_MODEL, D_FF), D_MODEL),
        'ffn1_w_down': w(ks[6], (DEPTH, D_FF, D_MODEL), D_FF, 0.5),
        'mix_norm': gain(ks[7], (DEPTH, D_MODEL)),
        'w_in': w(ks[8], (DEPTH, D_MODEL, IN_WIDTH), D_MODEL),
        'ret_gn': gain(ks[9], (DEPTH, R_WIDTH)),
        'w_branch_a': w(ks[10], (DEPTH, A_WIDTH, D_MODEL), A_WIDTH),
        'w_branch_b': w(ks[11], (DEPTH, R_WIDTH, D_MODEL), R_WIDTH),
        'w_out': w(ks[12], (DEPTH, D_MODEL, D_MODEL), D_MODEL, 0.5),
        'ffn2_norm': gain(ks[13], (DEPTH, D_MODEL)),
        'ffn2_w_gate': w(ks[14], (DEPTH, D_MODEL, D_FF), D_MODEL),
        'ffn2_w_up': w(ks[15], (DEPTH, D_MODEL, D_FF), D_MODEL),
        'ffn2_w_down': w(ks[16], (DEPTH, D_FF, D_MODEL), D_FF, 0.5),
        'ple_norm': gain(ks[17], (DEPTH, D_MODEL)),
        'w_ple_gate': w(ks[18], (DEPTH, D_MODEL, D_MODEL), D_MODEL),
        'w_ple_proj': w(ks[19], (DEPTH, PLE_DIM, D_MODEL), PLE_DIM, 0.5),
        'final_norm': gain(ks[20], (D_MODEL,)),
    }


def reference(x, p, positions, ffn1_norm, ffn1_w_gate, ffn1_w_up, ffn1_w_down,
              mix_norm, w_in, ret_gn, w_branch_a, w_branch_b, w_out,
              ffn2_norm, ffn2_w_gate, ffn2_w_up, ffn2_w_down,
              ple_norm, w_ple_gate, w_ple_proj, final_norm):
    b, s, _ = x.shape
    n_sel = min(TOPK_MAX, s // 4)
    split_points = []
    acc = 0
    for sz in IN_SPLITS[:-1]:
        acc += sz
        split_points.append(acc)

    h = x
    for i in range(DEPTH):
        h = h + 0.5 * swiglu(rms_norm(h, ffn1_norm[i]), ffn1_w_gate[i], ffn1_w_up[i], ffn1_w_down[i])

        u = rms_norm(h, mix_norm[i])
        z = u @ w_in[i]
        (aq, ak, av, iq, ik, iw, rq, rk, rv, rg, ga, gb) = jnp.split(z, split_points, axis=-1)

        aq = rope(aq.reshape(b, s, A_HEADS, A_HEAD_DIM), positions)
        ak = rope(ak[:, :, None, :], positions)[:, :, 0, :]
        iq = rope(iq.reshape(b, s, IDX_HEADS, IDX_DIM), positions) * (IDX_DIM ** -0.5)
        ik = rope(ik[:, :, None, :], positions)[:, :, 0, :]
        iw = iw * (IDX_HEADS ** -0.5)
        y_a = dsa_attention(aq, ak, av, iq, ik, iw, n_sel)

        rq = rope(rq.reshape(b, s, R_HEADS, R_QK_DIM), positions).astype(jnp.float32)
        rk = rope(rk.reshape(b, s, R_HEADS, R_QK_DIM), positions).astype(jnp.float32)
        rv = rv.reshape(b, s, R_HEADS, R_V_DIM).astype(jnp.float32)
        y_r = group_norm_heads(retention(rq, rk, rv), ret_gn[i]).astype(x.dtype) * jax.nn.silu(rg)

        merged = jax.nn.sigmoid(ga) * (y_a @ w_branch_a[i]) + jax.nn.sigmoid(gb) * (y_r @ w_branch_b[i])
        h = h + merged @ w_out[i]

        h = h + 0.5 * swiglu(rms_norm(h, ffn2_norm[i]), ffn2_w_gate[i], ffn2_w_up[i], ffn2_w_down[i])

        gate = jax.nn.sigmoid(rms_norm(h, ple_norm[i]) @ w_ple_gate[i])
        h = h + gate * (p[i] @ w_ple_proj[i])

    return rms_norm(h, final_norm)
```

```python
import math
import numpy as np
import concourse.bass as bass
import concourse.mybir as mybir
from concourse.bass_utils import run_bass_kernel_spmd

F32 = mybir.dt.float32
BF16 = mybir.dt.bfloat16
I32 = mybir.dt.int32
U8 = mybir.dt.uint8
AF = mybir.ActivationFunctionType
ALU = mybir.AluOpType
AX = mybir.AxisListType

D = 1024
DFF = 2816
NFC = DFF // 128
SEQ = 2048
BATCH = 16
NCORES = 8
T = 512
NG_SEQ = SEQ // T
SEQ_PER_CORE = BATCH // NCORES
TOK_CORE = SEQ_PER_CORE * SEQ
EPS = 1e-6
GN_EPS = 1e-5
KBIS = 20
NEG = -1.0e30
SLOT = 4096
NSLOT = 3

O_AQ, O_AK, O_AV, O_IQ, O_IK, O_IW = 0, 512, 576, 640, 1152, 1216
O_RQ, O_RK, O_RV, O_RG, O_GA, O_GB = 1224, 1736, 2248, 3272, 4296, 5320


class Op:
    __slots__ = ("eng", "fn", "deps", "needed", "event", "chan", "idx")

    def __init__(self, eng, fn, chan):
        self.eng = eng
        self.fn = fn
        self.deps = {}
        self.needed = False
        self.event = None
        self.chan = chan


class Prog:
    ENGS = ("pe", "act", "dve", "pool", "sp")

    def __init__(self):
        self.streams = {e: [] for e in self.ENGS}
        self.acc = {}
        self.const = {}
        self.chans = {}

    @staticmethod
    def rect(ap):
        t = ap.tensor
        name = t.name
        esz = mybir.dt.size(ap.dtype)
        dims = [(int(s), int(c)) for s, c in ap.ap]
        off = int(ap.offset)
        shp = [int(x) for x in t.shape]
        if str(ap.space) == "DRAM" or len(shp) == 1:
            ext = 1 + sum((c - 1) * abs(s) for s, c in dims)
            return name, (0, 1, off * esz, (off + ext) * esz)
        row = 1
        for x in shp[1:]:
            row *= x
        p0 = off // row
        c0 = off % row
        if dims[0][0] == row or dims[0][1] == 1:
            pc = dims[0][1]
            rest = dims[1:]
        elif dims[0][0] == 0:
            pc = 1
            rest = dims[1:]
        else:
            pc = 1
            rest = dims
        ext = 1 + sum((c - 1) * abs(s) for s, c in rest)
        if str(ap.space) == "PSUM":
            b0 = (c0 * esz) // 2048 * 2048
            b1 = ((c0 + ext) * esz + 2047) // 2048 * 2048
            return name, (p0 // 32 * 32, (p0 + pc + 31) // 32 * 32, b0, b1)
        return name, (p0, p0 + pc, c0 * esz, (c0 + ext) * esz)

    @staticmethod
    def ov(a, b):
        return a[0] < b[1] and b[0] < a[1] and a[2] < b[3] and b[2] < a[3]

    @staticmethod
    def covers(a, b):
        return a[0] <= b[0] and a[1] >= b[1] and a[2] <= b[2] and a[3] >= b[3]

    def add(self, eng, fn, outs=(), ins=(), chan=None, after=()):
        op = Op(eng, fn, chan)
        for a in after:
            op.deps[a] = "raw"
        rin = [self.rect(a) for a in ins]
        rout = [self.rect(a) for a in outs]
        for name, r in rin:
            if name in self.const:
                op.deps[self.const[name]] = "raw"
                continue
            for e in self.acc.get(name, ()):
                if e[2] and self.ov(e[0], r):
                    op.deps[e[1]] = "raw"
        for name, r in rout:
            for e in self.acc.get(name, ()):
                if self.ov(e[0], r):
                    if e[1] not in op.deps:
                        op.deps[e[1]] = "other"
        for name, r in rin:
            if name in self.const:
                continue
            lst = self.acc.setdefault(name, [])
            if chan is None:
                lst[:] = [e for e in lst if not ((not e[2]) and e[1].eng == eng and e[1].chan is None and e[0] == r)]
            lst.append((r, op, False))
        for name, r in rout:
            lst = self.acc.setdefault(name, [])
            lst[:] = [e for e in lst if not self.covers(r, e[0])]
            lst.append((r, op, True))
        op.deps.pop(op, None)
        self.streams[eng].append(op)
        return op

    def mark_const(self, name, op):
        self.const[name] = op
        self.acc.pop(name, None)

    def needs_wait(self, op, dep, kind):
        if dep.chan is not None:
            return True
        if op.chan is not None:
            return True
        if dep.eng != op.eng:
            return True
        if op.eng == "pe":
            return False
        return True

    def finalize(self):
        for e in self.ENGS:
            for op in self.streams[e]:
                for dep, kind in op.deps.items():
                    if self.needs_wait(op, dep, kind):
                        dep.needed = True
        cnt = {e: 0 for e in self.ENGS}
        ccnt = {}
        for e in self.ENGS:
            for op in self.streams[e]:
                if op.chan is not None:
                    ccnt[op.chan] = ccnt.get(op.chan, 0) + 16
                    op.event = (("c", op.chan), ccnt[op.chan])
                elif op.needed:
                    cnt[e] += 1
                    op.event = (("e", e), cnt[e])
        self.chan_list = sorted(ccnt.keys(), key=str)
        self.chan_final = ccnt

    def emit_stream(self, eng_name, eng, sems, final_waits=()):
        hw = {}
        for op in self.streams[eng_name]:
            waits = {}
            for dep, kind in op.deps.items():
                if not self.needs_wait(op, dep, kind):
                    continue
                s, v = dep.event
                if hw.get(s, 0) < v and waits.get(s, 0) < v:
                    waits[s] = v
            for s, v in waits.items():
                eng.wait_ge(sems[s], v)
                hw[s] = v
            ins = op.fn(eng)
            if op.chan is not None:
                ins.then_inc(sems[("c", op.chan)], 16)
            elif op.needed:
                ins.then_inc(sems[("e", eng_name)], 1)
        for s in final_waits:
            eng.wait_ge(sems[s], self.chan_final[s[1]])


def _fm_unit(W, cols):
    sub = W[:, cols]
    return sub.reshape(8, 128, len(cols)).transpose(1, 0, 2).reshape(128, -1)


def _swap_cols(c0, hd, n):
    cols = np.arange(c0, c0 + n)
    r = (cols - c0) % hd
    half = hd // 2
    return np.where(r < half, cols + half, cols - half)


class Packer:
    def __init__(self):
        self.parts = []
        self.loads = {}
        self.off = 0

    def add(self, name, arr):
        arr = np.ascontiguousarray(arr, dtype=np.float32)
        assert arr.shape[0] == 128
        e = arr.shape[1]
        assert e % 16 == 0 and e <= SLOT, (name, e)
        self.loads[name] = (self.off, e)
        self.parts.append(arr.reshape(-1))
        self.off += arr.size

    def flat(self):
        return np.concatenate(self.parts)


def pack_weights(inp):
    pk = Packer()
    w_in = inp["w_in"][0]
    def pack_ffn(f):
        wg, wu, wd = inp[f"ffn{f}_w_gate"][0], inp[f"ffn{f}_w_up"][0], inp[f"ffn{f}_w_down"][0]
        for i in range(NFC // 2):
            us = []
            for fc in (2 * i, 2 * i + 1):
                cols = np.arange(fc * 128, fc * 128 + 128)
                us += [_fm_unit(wg, cols), _fm_unit(wu, cols)]
            pk.add(f"f{f}gu{i}", np.concatenate(us, axis=1))
        for oc in range(8):
            u = wd[:, oc * 128:(oc + 1) * 128].reshape(NFC, 128, 128).transpose(1, 0, 2).reshape(128, -1)
            pk.add(f"f{f}d{oc}", u)

    pack_ffn(1)
    rope_chunks = []
    for j in range(4):
        rope_chunks.append((O_AQ + 128 * j, 64, False))
    for j in range(4):
        rope_chunks.append((O_IQ + 128 * j, 64, False))
    rope_chunks.append((O_AK, 64, True))
    rope_chunks.append((O_IK, 64, True))
    for j in range(4):
        rope_chunks.append((O_RQ + 128 * j, 128, False))
    for j in range(4):
        rope_chunks.append((O_RK + 128 * j, 128, False))
    units = []
    for c0, hd, dup in rope_chunks:
        if dup:
            cols = np.concatenate([np.arange(c0, c0 + 64), np.arange(c0, c0 + 64)])
            sw = _swap_cols(c0, 64, 64)
            scols = np.concatenate([sw, sw])
        else:
            cols = np.arange(c0, c0 + 128)
            scols = _swap_cols(c0, hd, 128)
        units.append(np.concatenate([_fm_unit(w_in, cols), _fm_unit(w_in, scols)], axis=1))
    for i in range(9):
        pk.add(f"rope{i}", np.concatenate(units[2 * i:2 * i + 2], axis=1))
    avw = np.concatenate([np.arange(O_AV, O_AV + 64), np.arange(O_IW, O_IW + 8), np.arange(O_IW, O_IW + 8)])
    pk.add("avw", _fm_unit(w_in, avw))
    for i in range(2):
        pk.add(f"rv{i}", _fm_unit(w_in, np.arange(O_RV + 512 * i, O_RV + 512 * i + 512)))
    for i in range(2):
        pk.add(f"rg{i}", _fm_unit(w_in, np.arange(O_RG + 512 * i, O_RG + 512 * i + 512)))
    wa, wb, wo = inp["w_branch_a"][0], inp["w_branch_b"][0], inp["w_out"][0]
    for oc in range(8):
        cols = np.arange(oc * 128, oc * 128 + 128)
        ua = np.zeros((128, 8, 128), np.float32)
        for hh in range(8):
            hp, j2 = hh // 4, hh % 4
            h = 2 * j2 + hp
            ua[0:64, hh, :] = wa[h * 64:(h + 1) * 64, oc * 128:(oc + 1) * 128]
        pk.add(f"mg{oc}", np.concatenate([
            _fm_unit(w_in, O_GA + cols), _fm_unit(wb, cols), _fm_unit(w_in, O_GB + cols), ua.reshape(128, -1)], axis=1))
    for i in range(2):
        pk.add(f"wo{i}", np.concatenate([_fm_unit(wo, np.arange(oc * 128, oc * 128 + 128)) for oc in range(4 * i, 4 * i + 4)], axis=1))
    pack_ffn(2)
    wpg, wpp = inp["w_ple_gate"][0], inp["w_ple_proj"][0]
    for i in range(2):
        pk.add(f"pg{i}", np.concatenate([_fm_unit(wpg, np.arange(oc * 128, oc * 128 + 128)) for oc in range(4 * i, 4 * i + 4)], axis=1))
    upp = wpp.reshape(2, 128, 1024).transpose(1, 0, 2).reshape(128, -1)
    pk.add("pp", upp)
    return pk


def make_consts(inp):
    c = {}
    c["ident"] = np.eye(128, dtype=np.float32)
    c["sel65"] = np.zeros((128, 64), np.float32)
    c["sel65"][64, :] = 1.0
    h = np.arange(4, dtype=np.float64)
    gam = 1.0 - np.exp2(-5.0 - h)
    lg = np.log(gam)
    p = np.arange(128)[:, None].astype(np.float64)
    dec = np.zeros((128, 4, 640), np.float64)
    sc = 128.0 ** -0.5
    cc = np.arange(128)[None, :].astype(np.float64)
    vis = ((p < 64) | (cc >= 64)).astype(np.float64)
    m = np.arange(512)[None, :].astype(np.float64)
    for hh in range(4):
        dec[:, hh, 0:128] = sc * np.exp(lg[hh] * np.abs(cc - p)) * vis
        dec[:, hh, 128:640] = sc * np.exp(lg[hh] * (m - p))
    c["decay"] = dec.reshape(128, -1).astype(np.float32)
    c["gam"] = gam
    gains = np.zeros((128, 40), np.float32)
    for i, k in enumerate(["ffn1_norm", "mix_norm", "ffn2_norm", "ple_norm"]):
        gains[:, i * 8:(i + 1) * 8] = inp[k][0].reshape(8, 128).T
    gains[:, 32:40] = inp["final_norm"].reshape(8, 128).T
    c["gains"] = gains
    sm = np.zeros((128, 8), np.float32)
    pp = np.arange(128)
    inv64 = (np.float32(10000.0) ** (-(np.arange(32, dtype=np.float32) / np.float32(32)))).astype(np.float32)
    inv128 = (np.float32(10000.0) ** (-(np.arange(64, dtype=np.float32) / np.float32(64)))).astype(np.float32)
    sm[:, 0] = inv64[pp % 32]
    sm[:, 1] = inv128[pp % 64]
    sm[:, 2] = np.where((pp % 64) < 32, -1.0, 1.0)
    sm[:, 3] = np.where(pp < 64, -1.0, 1.0)
    sm[:, 4] = EPS
    sm[:, 5] = GN_EPS
    c["small"] = sm
    c["pow2"] = np.tile((2.0 ** -(np.arange(KBIS + 2, dtype=np.float64) + 1.0))[None, :], (128, 1)).astype(np.float32)
    c["gn_g"] = np.ascontiguousarray(inp["ret_gn"][0], dtype=np.float32)
    return c


def build_program(loads, nw, gam, n_groups=2 * NG_SEQ, debug=None, stop=None):
    nc = bass.Bass("TRN2", target_bir_lowering=False)
    P = Prog()
    dbg_out = {}

    xT = nc.dram_tensor("xT", [D, TOK_CORE], F32, kind="ExternalInput").ap()
    pT = nc.dram_tensor("pT", [256, TOK_CORE], F32, kind="ExternalInput").ap()
    posd = nc.dram_tensor("pos", [1, TOK_CORE], I32, kind="ExternalInput").ap()
    wflat = nc.dram_tensor("wflat", [nw], F32, kind="ExternalInput").ap()
    wbf = nc.dram_tensor("wbf", [nw], BF16, kind="Internal").ap()
    c_ident = nc.dram_tensor("c_ident", [128, 128], F32, kind="ExternalInput").ap()
    c_sel = nc.dram_tensor("c_sel65", [128, 64], F32, kind="ExternalInput").ap()
    c_decay = nc.dram_tensor("c_decay", [128, 2560], F32, kind="ExternalInput").ap()
    c_gains = nc.dram_tensor("c_gains", [128, 40], F32, kind="ExternalInput").ap()
    c_small = nc.dram_tensor("c_small", [128, 8], F32, kind="ExternalInput").ap()
    c_pow2 = nc.dram_tensor("c_pow2", [128, KBIS + 2], F32, kind="ExternalInput").ap()
    c_gng = nc.dram_tensor("c_gn_g", [1024], F32, kind="ExternalInput").ap()
    outT = nc.dram_tensor("outT", [D, TOK_CORE], F32, kind="ExternalOutput").ap()
    if debug:
        for nm in debug:
            dbg_out[nm] = nc.dram_tensor("dbg_" + nm, [128, 8 * T], F32, kind="ExternalOutput").ap()

    from contextlib import ExitStack
    es = ExitStack()

    def sb(name, cols, dt):
        return es.enter_context(nc.sbuf_tensor(name, [128, cols], dt))

    hT_t = sb("hT", 8 * T, F32)
    uT_t = sb("uT", 8 * T, BF16)
    rstd_t = sb("rstd", T, F32)
    sqr_t = sb("sqr", 2 * T, BF16)
    R1 = sb("R1", NFC * T * 2, U8)
    yrT_t = sb("yrT", 8 * T, BF16)
    akT_t = sb("akT", SEQ, BF16)
    ikT_t = sb("ikT", SEQ, BF16)
    avx_t = sb("avx", 16 * 66, BF16)
    rkT_t = sb("rkT", 4 * SEQ, BF16)
    rv_t = sb("rv", 16 * 1024, BF16)
    C64 = sb("C64", T, F32)
    S64 = sb("S64", T, F32)
    C128 = sb("C128", T, F32)
    S128 = sb("S128", T, F32)
    identf_t = sb("identf", 128, F32)
    ident_t = sb("ident", 128, BF16)
    ones_t = sb("ones", 128, BF16)
    sel_t = sb("sel65", 64, F32)
    decay_t = sb("decay", 2560, F32)
    gng_t = sb("gn_g", 1024, F32)
    gains_t = sb("gains", 40, F32)
    small_t = sb("small", 8, F32)
    pow2_t = sb("pow2", KBIS + 2, F32)
    wring = sb("wring", NSLOT * SLOT, BF16)
    R2 = sb("R2", 35840 + 8192, U8)
    ps_t = es.enter_context(nc.psum_tensor("ps", [128, 4096], F32))

    hT = hT_t[:, :].rearrange("p (a t) -> p a t", a=8)
    uT = uT_t[:, :].rearrange("p (a t) -> p a t", a=8)
    yrT = yrT_t[:, :].rearrange("p (a t) -> p a t", a=8)
    aT = R1[:, :].bitcast(BF16).rearrange("p (a t) -> p a t", a=NFC)
    aqT = R1[:, 0:4096].bitcast(BF16).rearrange("p (a t) -> p a t", a=4)
    iqT = R1[:, 4096:8192].bitcast(BF16).rearrange("p (a t) -> p a t", a=4)
    rqT = R1[:, 8192:12288].bitcast(BF16).rearrange("p (a t) -> p a t", a=4)
    yaT = R1[:, 12288:20480].bitcast(BF16).rearrange("p (a t) -> p a t", a=8)
    avx = avx_t[:, :].rearrange("p (a t) -> p a t", a=16)
    rkT = rkT_t[:, :].rearrange("p (a t) -> p a t", a=4)
    rv = rv_t[:, :].rearrange("p (a t) -> p a t", a=16)
    decay = decay_t[:, :].rearrange("p (a t) -> p a t", a=4)

    def r2(off, nbytes, dt):
        return R2[:, off:off + nbytes].bitcast(dt)

    def bank(b, n=512, p0=0, p1=128):
        return ps_t[p0:p1, b * 512:b * 512 + n]

    def bank_bf(b, half, n=512, p0=0, p1=128):
        v = ps_t[p0:p1, (b + half) * 512:(b + half) * 512 + 256].bitcast(BF16)
        return v[:, 0:n]

    def dma(eng, out, in_, chan, after=()):
        return P.add(eng, lambda e, o=out, i=in_: e.dma_start(out=o, in_=i), outs=[out], ins=[in_], chan=chan, after=after)

    CH = 512 * 2048
    nch = (nw + CH - 1) // CH
    w2 = wflat.rearrange("(r c) -> r c", c=2048)
    wb2 = wbf.rearrange("(r c) -> r c", c=2048)
    cast_ops = []

    def emit_casts():
        for i in range(nch):
            r0 = i * 512
            r1_ = min(nw // 2048, r0 + 512)
            cast_ops.append(dma("pool", wb2[r0:r1_, :], w2[r0:r1_, :], ("cast", i),
                                after=cast_ops[i - 6:i - 5] if i >= 6 else ()))

    xstage = R2[:, 20480:20480 + 16384].bitcast(F32).rearrange("p (a t) -> p a t", a=8)

    def load_x(tok0):
        for kc in range(8):
            dma("pool", xstage[:, kc, :], xT[kc * 128:(kc + 1) * 128, tok0:tok0 + T], ("x", kc))

    cl = []
    cl.append(("identf", dma("sp", identf_t[:, :], c_ident[:, :], ("k", 0))))
    cl.append(("sel65", dma("sp", sel_t[:, :], c_sel[:, :], ("k", 1))))
    cl.append(("decay", dma("sp", decay_t[:, :], c_decay[:, :], ("k", 2))))
    cl.append(("gains", dma("sp", gains_t[:, :], c_gains[:, :], ("k", 3))))
    cl.append(("small", dma("sp", small_t[:, :], c_small[:, :], ("k", 4))))
    cl.append(("pow2", dma("sp", pow2_t[:, :], c_pow2[:, :], ("k", 5))))
    cl.append(("gn_g", dma("sp", gng_t[:, :], c_gng.partition_broadcast(128), ("k", 6))))
    for nm, op in cl:
        if nm != "identf":
            P.mark_const(nm, op)
    o1 = P.add("dve", lambda e: e.tensor_copy(out=ident_t[:, :], in_=identf_t[:, :]), outs=[ident_t[:, :]], ins=[identf_t[:, :]])
    o2 = P.add("dve", lambda e: e.memset(ones_t[:, :], 1.0), outs=[ones_t[:, :]])
    o3 = P.add("dve", lambda e: e.memset(avx_t[:, :], 1.0), outs=[avx_t[:, :]])
    P.mark_const("ident", o1)
    P.mark_const("ones", o2)

    ident = ident_t[:, :]
    ones = ones_t[:, :]

    def sm(col):
        return small_t[:, col:col + 1]

    ring_state = {"n": 0}

    def wload(name):
        off, e = loads[name]
        s = ring_state["n"] % NSLOT
        ring_state["n"] += 1
        dst = wring[:, s * SLOT:s * SLOT + e]
        src = wbf[off:off + 128 * e].rearrange("(p e) -> p e", p=128)
        dma("sp", dst, src, ("w", s))
        return dst

    def mm(out, lhsT, rhs, start, stop):
        P.add("pe", lambda e, o=out, l=lhsT, r=rhs, s=start, t=stop: e.matmul(o, lhsT=l, rhs=r, start=s, stop=t),
              outs=[out], ins=[lhsT, rhs])

    def tr(out, in_):
        P.add("pe", lambda e, o=out, i=in_: e.transpose(o, i, ident), outs=[out], ins=[in_])

    def act(out, in_, func, scale=None, bias=None, accum_out=None, extra_ins=()):
        kw = {}
        if scale is not None:
            kw["scale"] = scale
        if bias is not None:
            kw["bias"] = bias
        if accum_out is not None:
            kw["accum_out"] = accum_out
        outs = [out] + ([accum_out] if accum_out is not None else [])
        aps = [x for x in (scale, bias) if x is not None and not isinstance(x, (int, float))]
        P.add("act", lambda e, o=out, i=in_, f=func, k=kw: e.activation(out=o, in_=i, func=f, **k),
              outs=outs, ins=[in_] + aps + list(extra_ins))

    def tt(eng, out, in0, in1, op):
        P.add(eng, lambda e, o=out, a=in0, b=in1, p=op: e.tensor_tensor(out=o, in0=a, in1=b, op=p),
              outs=[out], ins=[in0, in1])

    def ts(eng, out, in0, s1, op0, s2=None, op1=None, accum_out=None):
        ins = [in0] + [x for x in (s1, s2) if not isinstance(x, (int, float, type(None)))]
        outs = [out] + ([accum_out] if accum_out is not None else [])
        kw = {}
        if op1 is not None:
            kw["op1"] = op1
        if accum_out is not None:
            kw["accum_out"] = accum_out
        P.add(eng, lambda e, o=out, a=in0, x=s1, y=s2, p=op0, k=kw: e.tensor_scalar(out=o, in0=a, scalar1=x, scalar2=y, op0=p, **k),
              outs=outs, ins=ins)

    def stt(out, in0, scalar, in1, op0, op1):
        ins = [in0, in1] + ([scalar] if not isinstance(scalar, (int, float)) else [])
        P.add("dve", lambda e, o=out, a=in0, s=scalar, b=in1, p=op0, q=op1: e.scalar_tensor_tensor(out=o, in0=a, scalar=s, in1=b, op0=p, op1=q),
              outs=[out], ins=ins)

    def cp(eng, out, in_):
        P.add(eng, lambda e, o=out, i=in_: e.tensor_copy(out=o, in_=i), outs=[out], ins=[in_])

    def memset(eng, out, val):
        P.add(eng, lambda e, o=out, v=val: e.memset(o, v), outs=[out])

    def dump(nm, ap3):
        if debug and nm in dbg_out:
            a = ap3.shape[1]
            dst = dbg_out[nm][:, 0:a * T].rearrange("p (a t) -> p a t", a=a)
            dma("pool", dst, ap3, ("dbg", nm))

    def rmsnorm_stats(src=None):
        src = hT if src is None else src
        ssq = bank(7)
        for kc in range(8):
            sq = sqr_t[:, (kc % 2) * T:(kc % 2 + 1) * T]
            act(sq, src[:, kc, :], AF.Square)
            mm(ssq, ones, sq, kc == 0, kc == 7)
        act(rstd_t[:, :], ssq, AF.Sqrt, scale=1.0 / D, bias=sm(4), extra_ins=[])
        P.add("dve", lambda e: e.reciprocal(out=rstd_t[:, :], in_=rstd_t[:, :]), outs=[rstd_t[:, :]], ins=[rstd_t[:, :]])

    def rmsnorm(gi, src=None, dst=None):
        rmsnorm_stats(src)
        src = hT if src is None else src
        dst = uT if dst is None else dst
        for kc in range(8):
            stt(dst[:, kc, :], src[:, kc, :], gains_t[:, gi * 8 + kc:gi * 8 + kc + 1], rstd_t[:, :], ALU.mult, ALU.mult)

    def ffn(f, gi, src=None, uin=None):
        for _ in ffn_gen(f, gi, src, uin):
            pass

    def ffn_gen(f, gi, src=None, uin=None):
        if uin is None:
            rmsnorm(gi, src)
            uin = uT
        src = hT if src is None else src
        yield
        sg_slots = [r2(0, 2048, F32), r2(2048, 2048, F32)]
        k = 0
        for i in range(NFC // 2):
            w = wload(f"f{f}gu{i}").rearrange("p (u kc j) -> p u kc j", u=4, kc=8)
            for q in range(2):
                fc = 2 * i + q
                gps, ups = bank(fc % 2), bank(2 + fc % 2)
                for kc in range(8):
                    mm(gps, w[:, 2 * q, kc, :], uin[:, kc, :], kc == 0, kc == 7)
                for kc in range(8):
                    mm(ups, w[:, 2 * q + 1, kc, :], uin[:, kc, :], kc == 0, kc == 7)
                sg = sg_slots[fc % 2]
                act(sg, gps, AF.Silu)
                tt("dve", aT[:, fc, :], ups, sg, ALU.mult)
                yield
        for oc in range(8):
            w = wload(f"f{f}d{oc}").rearrange("p (fc j) -> p fc j", fc=NFC)
            yps = bank(4 + oc % 2)
            for fc in range(NFC):
                mm(yps, w[:, fc, :], aT[:, fc, :], fc == 0, fc == NFC - 1)
            stt(hT[:, oc, :], yps, 0.5, src[:, oc, :], ALU.mult, ALU.add)
            yield

    def rope_tables(tok0):
        for _ in rope_tables_gen(tok0):
            pass

    def rope_tables_gen(tok0):
        posi = r2(20480, 2048, I32)
        posf = r2(22528, 2048, F32)
        ang = r2(24576, 2048, F32)
        tmp = r2(26624, 2048, F32)
        tmi = r2(28672, 2048, I32)
        red = r2(30720, 2048, F32)
        dma("pool", posi, posd[0, tok0:tok0 + T].partition_broadcast(128), ("pos", 0))
        cp("dve", posf, posi)
        C1 = 6.28125
        C2 = 2.0 * math.pi - 6.28125
        for col, (Ct, St) in enumerate(((C64, S64), (C128, S128))):
            ts("dve", ang, posf, sm(col), ALU.mult)
            for shift, dst, is_sin in ((0.0, St, True), (math.pi / 2, Ct, False)):
                ts("dve", tmp, ang, shift, ALU.add, 1.0 / (2.0 * math.pi), ALU.mult)
                cp("dve", tmi, tmp)
                cp("dve", tmp, tmi)
                stt(red, tmp, -C1, ang, ALU.mult, ALU.add)
                stt(red, tmp, -C2, red, ALU.mult, ALU.add)
                if shift != 0.0:
                    ts("dve", red, red, shift, ALU.add)
                yield
                ts("dve", tmp, red, math.pi, ALU.is_gt, -2.0 * math.pi, ALU.mult)
                tt("dve", red, red, tmp, ALU.add)
                ts("dve", tmp, red, -math.pi, ALU.is_lt, 2.0 * math.pi, ALU.mult)
                tt("dve", red, red, tmp, ALU.add)
                ts("dve", red, red, 3.1415925, ALU.min, -3.1415925, ALU.max)
                if is_sin:
                    act(dst[:, :], red, AF.Sin, scale=sm(2 + col))
                else:
                    act(dst[:, :], red, AF.Sin)
                yield

    import os
    PE2 = os.environ.get("PE2", "dve")
    PIPE = int(os.environ.get("PIPE", "1"))
    ACT_SHARE = float(os.environ.get("ACT_SHARE", "0.6"))

    def mixer_proj(g):
        gs = g % NG_SEQ
        tok_s = gs * T
        dests = ([aqT[:, j, :] for j in range(4)] + [iqT[:, j, :] for j in range(4)] +
                 [akT_t[:, tok_s:tok_s + T], ikT_t[:, tok_s:tok_s + T]] +
                 [rqT[:, j, :] for j in range(4)] + [rkT[:, j, tok_s:tok_s + T] for j in range(4)])
        tabs = [(C64, S64)] * 10 + [(C128, S128)] * 8
        t1s = [r2(4096, 2048, F32), r2(6144, 2048, F32)]
        t2s = [r2(8192, 2048, F32), r2(10240, 2048, F32)]
        for i in range(9):
            w = wload(f"rope{i}").rearrange("p (u kc j) -> p u kc j", u=4, kc=8)
            for q in range(2):
                ci = 2 * i + q
                aps, bps = bank(ci % 2), bank(2 + ci % 2)
                for kc in range(8):
                    mm(aps, w[:, 2 * q, kc, :], uT[:, kc, :], kc == 0, kc == 7)
                for kc in range(8):
                    mm(bps, w[:, 2 * q + 1, kc, :], uT[:, kc, :], kc == 0, kc == 7)
                Ct, St = tabs[ci]
                t1, t2 = t1s[ci % 2], t2s[ci % 2]
                tt("dve", t1, aps, Ct[:, :], ALU.mult)
                tt("dve", t2, bps, St[:, :], ALU.mult)
                tt(PE2, dests[ci], t1, t2, ALU.add)
        w = wload("avw").rearrange("p (kc j) -> p kc j", kc=8)
        dg = r2(33792, 4 * 8 * 4, F32).rearrange("p (a t) -> p a t", a=4)
        for jt in range(4):
            tb = gs * 4 + jt
            pso = bank(4 + jt % 2, 80)
            for kc in range(8):
                mm(pso, uT[:, kc, jt * 128:(jt + 1) * 128], w[:, kc, :], kc == 0, kc == 7)
            act(avx[:, tb, 0:64], pso[:, 0:64], AF.Copy)
            ts("dve", dg[:, jt, :], pso[:, 64:72], (8.0 ** -0.5) * (64.0 ** -0.5), ALU.mult)
        for i in range(2):
            w = wload(f"rv{i}").rearrange("p (kc j) -> p kc j", kc=8)
            for jt in range(4):
                tb = gs * 4 + jt
                pso = bank(4 + jt % 2)
                for kc in range(8):
                    mm(pso, uT[:, kc, jt * 128:(jt + 1) * 128], w[:, kc, :], kc == 0, kc == 7)
                act(rv[:, tb, i * 512:(i + 1) * 512], pso, AF.Copy)
        return dg

    def attn_bufs():
        b = {}
        b["score"] = [r2(0, 8192, F32), r2(35840, 8192, F32)]
        b["mask"] = r2(8192, 4096, BF16)
        b["maskT"] = r2(12288, 4096, BF16).rearrange("p (a t) -> p a t", a=16)
        b["rels"] = [r2(16384 + 1024 * i, 1024, BF16) for i in range(3)]
        b["Es"] = [r2(19456 + 1024 * i, 1024, BF16) for i in range(3)]
        b["Ps"] = [r2(22528 + 1024 * i, 1024, BF16) for i in range(3)]
        b["oT"] = r2(25600, 4096, F32).rearrange("p (a t) -> p a t", a=2)
        b["rcp"] = r2(29696, 2048, F32)
        b["diag"] = r2(31744, 2048, BF16).rearrange("p (a t) -> p a t", a=8)
        b["st"] = r2(34304, 512, F32)
        return b

    def attn_A(g, jt, dg, B_):
        gs = g % NG_SEQ
        t = gs * 4 + jt
        nk = 128 * (t + 1)
        nchk = (nk + 511) // 512
        qc = slice(jt * 128, (jt + 1) * 128)
        score_sb = B_["score"][jt % 2]
        rels, diag = B_["rels"], B_["diag"]
        for h in range(8):
            ts("dve", diag[:, h, :], ident, dg[:, jt, h:h + 1], ALU.mult)
        steps = [(h, c) for h in range(8) for c in range(nchk)]

        def cw(c):
            return min(512, nk - 512 * c)

        def mm1(i):
            h, c = steps[i]
            hp, j2 = h % 2, h // 2
            n = cw(c)
            mm(bank(4 + i % 2, n), iqT[hp * 64:hp * 64 + 64, j2, qc], ikT_t[hp * 64:hp * 64 + 64, 512 * c:512 * c + n], True, True)
            if i % 3 != 2:
                act(rels[i % 3][:, 0:n], bank(4 + i % 2, n), AF.Relu)
            else:
                ts("dve", rels[i % 3][:, 0:n], bank(4 + i % 2, n), 0.0, ALU.max)

        def mm2(i):
            h, c = steps[i]
            n = cw(c)
            mm(bank(c, n), diag[:, h, :], rels[i % 3][:, 0:n], h == 0, h == 7)

        for i in range(len(steps) + 1):
            if i < len(steps):
                mm1(i)
            if i >= 1:
                mm2(i - 1)
            yield
        for c in range(nchk):
            n = cw(c)
            act(score_sb[:, 512 * c:512 * c + n], bank(c, n), AF.Copy)
        memset(PE2, score_sb[0:64, nk - 64:nk], NEG)
        yield

    def attn_B(g, jt, B_):
        gs = g % NG_SEQ
        t = gs * 4 + jt
        nk = 128 * (t + 1)
        score_sb = B_["score"][jt % 2]
        mask, maskT, st = B_["mask"], B_["maskT"], B_["st"]
        rmax, lo0, w0, mid, ssum, av_, thr, cntd, vv = [st[:, i:i + 1] for i in range(9)]
        wk = st[:, 12:12 + KBIS + 2]
        if t >= 2:
            P.add("dve", lambda e: e.tensor_reduce(out=rmax, in_=score_sb[:, 0:nk], axis=AX.X, op=ALU.max),
                  outs=[rmax], ins=[score_sb[:, 0:nk]])
            P.add("dve", lambda e: e.tensor_reduce(out=lo0, in_=score_sb[:, 0:256], axis=AX.X, op=ALU.min),
                  outs=[lo0], ins=[score_sb[:, 0:256]])
            tt("dve", w0, rmax, lo0, ALU.subtract)
            ts("dve", wk, pow2_t[:, :], w0, ALU.mult)
            tt("dve", mid, lo0, wk[:, 0:1], ALU.add)
            yield
            nA = int(round(nk * ACT_SHARE / 64.0)) * 64
            nA = max(64, min(nk - 64, nA))
            cthr = 255.75 - nA / 2.0
            for k in range(KBIS):
                act(mask[:, 0:nA], score_sb[:, 0:nA], AF.Sign, scale=-1.0, bias=mid, accum_out=ssum)
                ts("dve", mask[:, nA:nk], score_sb[:, nA:nk], mid, ALU.is_ge, None, ALU.add, accum_out=cntd)
                yield
                stt(vv, ssum, -0.5, cntd, ALU.mult, ALU.add)
                stt(av_, vv, cthr, wk[:, k:k + 1], ALU.is_ge, ALU.mult)
                stt(mid, mid, wk[:, k + 1:k + 2], av_, ALU.subtract, ALU.add)
                yield
            tt("dve", thr, mid, wk[:, KBIS:KBIS + 1], ALU.subtract)
        else:
            memset("dve", thr, -1.0e29)
        ts("dve", mask[:, 0:nk], score_sb[:, 0:nk], thr, ALU.is_ge)
        yield
        for kb0 in range(0, t + 1, 4):
            nb = min(4, t + 1 - kb0)
            half = (kb0 // 4) % 2
            for q in range(nb):
                kb = kb0 + q
                tr(bank_bf(6, half, 512)[:, q * 128:(q + 1) * 128], mask[:, kb * 128:(kb + 1) * 128])
            act(maskT[:, kb0:kb0 + nb, :], bank_bf(6, half, nb * 128).rearrange("p (a t) -> p a t", a=nb), AF.Copy)
            yield

    def attn_C(g, jt, B_):
        gs = g % NG_SEQ
        t = gs * 4 + jt
        qc = slice(jt * 128, (jt + 1) * 128)
        maskT, Es, Ps, oT_sb, rcp = B_["maskT"], B_["Es"], B_["Ps"], B_["oT"], B_["rcp"]
        asteps = [(kb, hp) for kb in range(t + 1) for hp in range(2)]

        def L(i):
            kb, hp = asteps[i]
            lps = bank(i % 3)
            mm(lps, akT_t[hp * 64:hp * 64 + 64, kb * 128:(kb + 1) * 128], aqT[hp * 64:hp * 64 + 64, :, qc], True, True)
            act(Es[i % 3], lps, AF.Exp, scale=0.125)
            e3 = Es[i % 3].rearrange("p (a t) -> p a t", a=4)
            p3 = Ps[i % 3].rearrange("p (a t) -> p a t", a=4)
            m3 = maskT[:, kb:kb + 1, :].broadcast_to([128, 4, 128])
            tt(PE2, p3, e3, m3, ALU.mult)

        def V(i):
            kb, hp = asteps[i]
            mm(bank(4 + hp, 512, 0, 65), avx[:, kb, 0:65], Ps[i % 3], kb == 0, kb == t)

        na = len(asteps)
        for i in range(na + 2):
            if i < na:
                L(i)
            if i >= 2:
                V(i - 2)
            yield
        for hp in range(2):
            act(oT_sb[0:65, hp, :], bank(4 + hp, 512, 0, 65), AF.Copy)
            bc = bank(3, 512, 0, 64)
            mm(bc, sel_t[0:65, :], oT_sb[0:65, hp, :], True, True)
            P.add("dve", lambda e, b=bc: e.reciprocal(out=rcp[0:64, :], in_=b), outs=[rcp[0:64, :]], ins=[bc])
            tt("dve", yaT[0:64, hp * 4:hp * 4 + 4, qc],
               oT_sb[0:64, hp, :].rearrange("p (a t) -> p a t", a=4),
               rcp[0:64, :].rearrange("p (a t) -> p a t", a=4), ALU.mult)
        yield

    def run_interleaved(gens):
        live = []
        for gen, n in gens:
            live.append([gen, 0, max(1, n)])
        while live:
            live.sort(key=lambda x: x[1] / x[2])
            cur = live[0]
            try:
                next(cur[0])
                cur[1] += 1
            except StopIteration:
                live.remove(cur)

    def attention_group(g, dg):
        gs = g % NG_SEQ
        B_ = attn_bufs()

        def nA(jt):
            nk = 128 * (gs * 4 + jt + 1)
            return 8 * ((nk + 511) // 512) + 2

        def nB(jt):
            t = gs * 4 + jt
            return (2 * KBIS + 3 if t >= 2 else 1) + (t + 4) // 4

        def nC(jt):
            return 2 * (gs * 4 + jt + 1) + 3

        def mk(kind, jt):
            if kind == "A":
                return attn_A(g, jt, dg, B_), nA(jt)
            if kind == "B":
                return attn_B(g, jt, B_), nB(jt)
            return attn_C(g, jt, B_), nC(jt)

        def chain(items):
            def gen():
                for kind, jt in items:
                    yield from mk(kind, jt)[0]
            return gen(), sum(mk(kind, jt)[1] for kind, jt in items)

        if PIPE:
            stages = [[[("A", 0)]], [[("B", 0)], [("A", 1)]], [[("B", 1)], [("C", 0), ("A", 2)]],
                      [[("B", 2)], [("C", 1), ("A", 3)]], [[("B", 3)], [("C", 2)]], [[("C", 3)]]]
        else:
            stages = [[[(k, jt)]] for jt in range(4) for k in "ABC"]
        for stg in stages:
            run_interleaved([chain(items) for items in stg])

    def retention(g):
        gs = g % NG_SEQ
        sds = [r2(1024 * i, 1024, BF16) for i in range(3)]
        grg = r2(3072, 16384, F32).rearrange("p (a t) -> p a t", a=4)
        yr = r2(19456, 8192, BF16).rearrange("p (a t) -> p a t", a=4)
        yn = [r2(27648 + 1024 * i, 1024, F32) for i in range(4)]
        st = r2(31744, 512, F32)
        wr = [wload(f"rg{i}").rearrange("p (kc j) -> p kc j", kc=8) for i in range(2)]
        for jt in range(4):
            for i in range(2):
                pso = bank(4 + i)
                for kc in range(8):
                    mm(pso, uT[:, kc, jt * 128:(jt + 1) * 128], wr[i][:, kc, :], kc == 0, kc == 7)
                act(grg[:, jt, i * 512:(i + 1) * 512], pso, AF.Silu)
            tt(PE2, grg[:, jt, :], grg[:, jt, :], gng_t[:, :], ALU.mult)
        nkb = 4 * gs + 4
        k = 0
        for h in range(4):
            def S(kb, i):
                r = max(0, kb - 4 * gs)
                n = (4 - r) * 128
                sps = bank(4 + i % 3, n)
                mm(sps, rkT[:, h, kb * 128:(kb + 1) * 128], rqT[:, h, r * 128:512], True, True)
                sd = sds[i % 3]
                if kb < 4 * gs:
                    gm = float(gam[h] ** (128.0 * (4 * gs - kb)))
                    stt(sd[:, 0:512], sps, gm, decay[:, h, 128:640], ALU.mult, ALU.mult)
                else:
                    tt("dve", sd[:, 0:128], sps[:, 0:128], decay[:, h, 0:128], ALU.mult)
                    if n > 128:
                        tt("dve", sd[:, 128:n], sps[:, 128:n], decay[:, h, 256:256 + n - 128], ALU.mult)

            def PV(kb, i):
                r = max(0, kb - 4 * gs)
                sd = sds[i % 3]
                for jt in range(r, 4):
                    mm(bank(jt, 256), sd[:, (jt - r) * 128:(jt - r + 1) * 128], rv[:, kb, h * 256:(h + 1) * 256],
                       kb == 0, kb == 4 * gs + jt)

            for i in range(nkb + 2):
                if i < nkb:
                    S(i, i)
                if i >= 2:
                    PV(i - 2, i - 2)
            bsts = [st[:, jt * 8:jt * 8 + 6] for jt in range(4)]
            mvs = [st[:, 32 + jt * 2:32 + jt * 2 + 2] for jt in range(4)]
            rss = [st[:, 40 + jt:41 + jt] for jt in range(4)]
            for jt in range(4):
                P.add("dve", lambda e, o=bsts[jt], i=bank(jt, 256): e.bn_stats(out=o, in_=i), outs=[bsts[jt]], ins=[bank(jt, 256)])
            for jt in range(4):
                P.add("dve", lambda e, o=mvs[jt], i=bsts[jt]: e.bn_aggr(out=o, in_=i), outs=[mvs[jt]], ins=[bsts[jt]])
            for jt in range(4):
                act(rss[jt], mvs[jt][:, 1:2], AF.Sqrt, bias=sm(5))
            for jt in range(4):
                P.add("dve", lambda e, o=rss[jt]: e.reciprocal(out=o, in_=o), outs=[rss[jt]], ins=[rss[jt]])
            nmr = [st[:, 48 + jt:49 + jt] for jt in range(4)]
            for jt in range(4):
                stt(nmr[jt], mvs[jt][:, 0:1], -1.0, rss[jt], ALU.mult, ALU.mult)
            for jt in range(4):
                act(yn[jt][:, 0:256], bank(jt, 256), AF.Identity, scale=rss[jt], bias=nmr[jt])
            for jt in range(4):
                tt(PE2, yr[:, jt, h * 256:(h + 1) * 256], yn[jt][:, 0:256], grg[:, jt, h * 256:(h + 1) * 256], ALU.mult)
        for kc in range(8):
            half = kc % 2
            pst = bank_bf(6, half, 512)
            for jt in range(4):
                tr(pst[:, jt * 128:(jt + 1) * 128], yr[:, jt, kc * 128:(kc + 1) * 128])
            act(yrT[:, kc, :], pst, AF.Copy)

    def merge(g):
        m1s = [r2(0, 2048, F32), r2(2048, 2048, F32)]
        sgs = [r2(4096, 2048, F32), r2(6144, 2048, F32)]
        m2s = [r2(8192, 2048, F32), r2(10240, 2048, F32)]
        sg2 = [r2(12288, 2048, F32), r2(14336, 2048, F32)]
        mT = r2(16384, 8192, BF16).rearrange("p (a t) -> p a t", a=8)
        for oc in range(8):
            w = wload(f"mg{oc}").rearrange("p (u kc j) -> p u kc j", u=4, kc=8)
            q = oc % 2
            aps, gaps, bps, gbps = bank(q), bank(2 + q), bank(4 + q), bank(6 + q)
            for hh in range(8):
                mm(aps, w[0:64, 3, hh, :], yaT[0:64, hh, :], hh == 0, hh == 7)
            for kc in range(8):
                mm(gaps, w[:, 0, kc, :], uT[:, kc, :], kc == 0, kc == 7)
            for kc in range(8):
                mm(bps, w[:, 1, kc, :], yrT[:, kc, :], kc == 0, kc == 7)
            for kc in range(8):
                mm(gbps, w[:, 2, kc, :], uT[:, kc, :], kc == 0, kc == 7)
            act(sgs[q], gaps, AF.Sigmoid)
            act(sg2[q], gbps, AF.Sigmoid)
            tt("dve", m1s[q], aps, sgs[q], ALU.mult)
            tt("dve", m2s[q], bps, sg2[q], ALU.mult)
            tt(PE2, mT[:, oc, :], m1s[q], m2s[q], ALU.add)
        for i in range(2):
            w = wload(f"wo{i}").rearrange("p (u kc j) -> p u kc j", u=4, kc=8)
            for u in range(4):
                oc = 4 * i + u
                ops_ = bank(oc % 2)
                for kc in range(8):
                    mm(ops_, w[:, u, kc, :], mT[:, kc, :], kc == 0, kc == 7)
                tt("dve", hT[:, oc, :], ops_, hT[:, oc, :], ALU.add)

    def ple(g, tok0):
        pTf = r2(0, 4096, F32).rearrange("p (a t) -> p a t", a=2)
        pTb = r2(4096, 2048, BF16).rearrange("p (a t) -> p a t", a=2)
        pgs = [r2(8192, 2048, F32), r2(10240, 2048, F32)]
        ptm = [r2(12288, 2048, F32), r2(14336, 2048, F32)]
        dma("pool", pTf, pT[:, tok0:tok0 + T].rearrange("(a p) t -> p a t", p=128), ("p", 0))
        act(pTb, pTf, AF.Copy)
        rmsnorm(3)
        wpp = None
        for i in range(2):
            w = wload(f"pg{i}").rearrange("p (u kc j) -> p u kc j", u=4, kc=8)
            if wpp is None:
                wpp = wload("pp").rearrange("p (kc j) -> p kc j", kc=2)
            for u in range(4):
                oc = 4 * i + u
                q = oc % 2
                gps, pps = bank(q), bank(2 + q)
                for kc in range(8):
                    mm(gps, w[:, u, kc, :], uT[:, kc, :], kc == 0, kc == 7)
                for kc in range(2):
                    mm(pps, wpp[:, kc, oc * 128:(oc + 1) * 128], pTb[:, kc, :], kc == 0, kc == 1)
                act(pgs[q], gps, AF.Sigmoid)
                tt("dve", ptm[q], pps, pgs[q], ALU.mult)
                tt(PE2, hT[:, oc, :], hT[:, oc, :], ptm[q], ALU.add)

    def final(g, tok0):
        ost = [r2(16384, 2048, F32), r2(18432, 2048, F32)]
        rmsnorm_stats()
        for oc in range(8):
            o = ost[oc % 2]
            stt(o, hT[:, oc, :], gains_t[:, 32 + oc:33 + oc], rstd_t[:, :], ALU.mult, ALU.mult)
            dma("pool", outT[oc * 128:(oc + 1) * 128, tok0:tok0 + T], o, ("out", oc % 2))

    for g in range(n_groups):
        tok0 = g * T
        if g == 0:
            load_x(tok0)
            emit_casts()
        if stop == "x":
            final(g, tok0)
            continue
        if stop == "norm":
            rmsnorm(0)
            final(g, tok0)
            continue
        if g == 0:
            rmsnorm(0, xstage, yrT)
        ffn(1, 0, xstage, yrT)
        if g == 0:
            dump("h1", hT)
        if stop == "ffn1":
            final(g, tok0)
            continue
        if g == 0:
            rope_tables(tok0)
        if stop == "rope":
            final(g, tok0)
            continue
        rmsnorm(1)
        dg = mixer_proj(g)
        if stop == "proj":
            final(g, tok0)
            continue
        attention_group(g, dg)
        if stop == "attn":
            final(g, tok0)
            continue
        retention(g)
        if stop == "ret":
            final(g, tok0)
            continue
        merge(g)
        if g == 0:
            dump("h2", hT)
        if stop == "merge":
            final(g, tok0)
            continue
        if g + 1 < n_groups:
            run_interleaved([(ffn_gen(2, 2), 31), (rope_tables_gen(tok0 + T), 9)])
            load_x(tok0 + T)
        else:
            ffn(2, 2)
        ple(g, tok0)
        if g == 0:
            dump("h3", hT)
        if g + 1 < n_groups:
            rmsnorm(0, xstage, yrT)
        final(g, tok0)

    P.finalize()
    sems = {}
    for e in Prog.ENGS:
        sems[("e", e)] = es.enter_context(nc.semaphore("sem_" + e))
    for ch in P.chan_list:
        sems[("c", ch)] = es.enter_context(nc.semaphore("semc_" + "_".join(str(x) for x in ch)))
    finals = [("c", ch) for ch in P.chan_list if ch[0] in ("out", "dbg")]
    block = es.enter_context(nc.Block())

    @block.tensor
    def _(e):
        P.emit_stream("pe", e, sems)

    @block.scalar
    def _(e):
        P.emit_stream("act", e, sems)

    @block.vector
    def _(e):
        P.emit_stream("dve", e, sems)

    @block.gpsimd
    def _(e):
        P.emit_stream("pool", e, sems, final_waits=finals)

    @block.sync
    def _(e):
        P.emit_stream("sp", e, sems)

    es.close()
    return nc


def _prepare(inputs):
    inp = {k: np.asarray(v) for k, v in inputs.items()}
    pk = pack_weights(inp)
    wflat = pk.flat()
    pad = (-wflat.size) % (512 * 2048)
    if pad:
        wflat = np.concatenate([wflat, np.zeros(pad, np.float32)])
    consts = make_consts(inp)
    x = inp["x"].astype(np.float32, copy=False)
    p = inp["p"][0].astype(np.float32, copy=False)
    pos = inp["positions"].astype(np.int32, copy=False)
    in_maps = []
    for c in range(NCORES):
        b0 = c * SEQ_PER_CORE
        xc = np.ascontiguousarray(x[b0:b0 + SEQ_PER_CORE].reshape(TOK_CORE, D).T)
        pc = np.ascontiguousarray(p[b0:b0 + SEQ_PER_CORE].reshape(TOK_CORE, 256).T)
        posc = np.ascontiguousarray(pos[b0:b0 + SEQ_PER_CORE].reshape(1, TOK_CORE))
        in_maps.append({
            "xT": xc, "pT": pc, "pos": posc, "wflat": wflat,
            "c_ident": consts["ident"], "c_sel65": consts["sel65"], "c_decay": consts["decay"],
            "c_gains": consts["gains"], "c_small": consts["small"], "c_pow2": consts["pow2"],
            "c_gn_g": consts["gn_g"],
        })
    return pk, wflat, consts, in_maps


def kernel(**inputs):
    pk, wflat, consts, in_maps = _prepare(inputs)
    nc = build_program(pk.loads, wflat.size, consts["gam"])
    res = run_bass_kernel_spmd(nc, in_maps, core_ids=list(range(NCORES)))
    out = np.empty((BATCH, SEQ, D), np.float32)
    for c in range(NCORES):
        oT = np.asarray(res.results[c]["outT"])
        out[c * SEQ_PER_CORE:(c + 1) * SEQ_PER_CORE] = oT.T.reshape(SEQ_PER_CORE, SEQ, D)
    return out
```

```python
import math
import numpy as np
import concourse.bass as bass
import concourse.mybir as mybir
from concourse.bass_utils import run_bass_kernel_spmd

F32 = mybir.dt.float32
BF16 = mybir.dt.bfloat16
I32 = mybir.dt.int32
U8 = mybir.dt.uint8
AF = mybir.ActivationFunctionType
ALU = mybir.AluOpType
AX = mybir.AxisListType

D = 1024
DFF = 2816
NFC = DFF // 128
SEQ = 2048
BATCH = 16
NCORES = 8
T = 512
NG_SEQ = SEQ // T
SEQ_PER_CORE = BATCH // NCORES
TOK_CORE = SEQ_PER_CORE * SEQ
EPS = 1e-6
GN_EPS = 1e-5
KBIS = 20
NEG = -1.0e30
SLOT = 4096
NSLOT = 3

O_AQ, O_AK, O_AV, O_IQ, O_IK, O_IW = 0, 512, 576, 640, 1152, 1216
O_RQ, O_RK, O_RV, O_RG, O_GA, O_GB = 1224, 1736, 2248, 3272, 4296, 5320


class Op:
    __slots__ = ("eng", "fn", "deps", "needed", "event", "chan", "idx")

    def __init__(self, eng, fn, chan):
        self.eng = eng
        self.fn = fn
        self.deps = {}
        self.needed = False
        self.event = None
        self.chan = chan


class Prog:
    ENGS = ("pe", "act", "dve", "pool", "sp")

    def __init__(self):
        self.streams = {e: [] for e in self.ENGS}
        self.acc = {}
        self.const = {}
        self.chans = {}

    @staticmethod
    def rect(ap):
        t = ap.tensor
        name = t.name
        esz = mybir.dt.size(ap.dtype)
        dims = [(int(s), int(c)) for s, c in ap.ap]
        off = int(ap.offset)
        shp = [int(x) for x in t.shape]
        if str(ap.space) == "DRAM" or len(shp) == 1:
            ext = 1 + sum((c - 1) * abs(s) for s, c in dims)
            return name, (0, 1, off * esz, (off + ext) * esz)
        row = 1
        for x in shp[1:]:
            row *= x
        p0 = off // row
        c0 = off % row
        if dims[0][0] == row or dims[0][1] == 1:
            pc = dims[0][1]
            rest = dims[1:]
        elif dims[0][0] == 0:
            pc = 1
            rest = dims[1:]
        else:
            pc = 1
            rest = dims
        ext = 1 + sum((c - 1) * abs(s) for s, c in rest)
        if str(ap.space) == "PSUM":
            b0 = (c0 * esz) // 2048 * 2048
            b1 = ((c0 + ext) * esz + 2047) // 2048 * 2048
            return name, (p0 // 32 * 32, (p0 + pc + 31) // 32 * 32, b0, b1)
        return name, (p0, p0 + pc, c0 * esz, (c0 + ext) * esz)

    @staticmethod
    def ov(a, b):
        return a[0] < b[1] and b[0] < a[1] and a[2] < b[3] and b[2] < a[3]

    @staticmethod
    def covers(a, b):
        return a[0] <= b[0] and a[1] >= b[1] and a[2] <= b[2] and a[3] >= b[3]

    def add(self, eng, fn, outs=(), ins=(), chan=None, after=()):
        op = Op(eng, fn, chan)
        for a in after:
            op.deps[a] = "raw"
        rin = [self.rect(a) for a in ins]
        rout = [self.rect(a) for a in outs]
        for name, r in rin:
            if name in self.const:
                op.deps[self.const[name]] = "raw"
                continue
            for e in self.acc.get(name, ()):
                if e[2] and self.ov(e[0], r):
                    op.deps[e[1]] = "raw"
        for name, r in rout:
            for e in self.acc.get(name, ()):
                if self.ov(e[0], r):
                    if e[1] not in op.deps:
                        op.deps[e[1]] = "other"
        for name, r in rin:
            if name in self.const:
                continue
            lst = self.acc.setdefault(name, [])
            if chan is None:
                lst[:] = [e for e in lst if not ((not e[2]) and e[1].eng == eng and e[1].chan is None and e[0] == r)]
            lst.append((r, op, False))
        for name, r in rout:
            lst = self.acc.setdefault(name, [])
            lst[:] = [e for e in lst if not self.covers(r, e[0])]
            lst.append((r, op, True))
        op.deps.pop(op, None)
        self.streams[eng].append(op)
        return op

    def mark_const(self, name, op):
        self.const[name] = op
        self.acc.pop(name, None)

    def needs_wait(self, op, dep, kind):
        if dep.chan is not None:
            return True
        if op.chan is not None:
            return True
        if dep.eng != op.eng:
            return True
        if op.eng == "pe":
            return False
        return True

    def finalize(self):
        for e in self.ENGS:
            for op in self.streams[e]:
                for dep, kind in op.deps.items():
                    if self.needs_wait(op, dep, kind):
                        dep.needed = True
        cnt = {e: 0 for e in self.ENGS}
        ccnt = {}
        for e in self.ENGS:
            for op in self.streams[e]:
                if op.chan is not None:
                    ccnt[op.chan] = ccnt.get(op.chan, 0) + 16
                    op.event = (("c", op.chan), ccnt[op.chan])
                elif op.needed:
                    cnt[e] += 1
                    op.event = (("e", e), cnt[e])
        self.chan_list = sorted(ccnt.keys(), key=str)
        self.chan_final = ccnt

    def emit_stream(self, eng_name, eng, sems, final_waits=()):
        hw = {}
        for op in self.streams[eng_name]:
            waits = {}
            for dep, kind in op.deps.items():
                if not self.needs_wait(op, dep, kind):
                    continue
                s, v = dep.event
                if hw.get(s, 0) < v and waits.get(s, 0) < v:
                    waits[s] = v
            for s, v in waits.items():
                eng.wait_ge(sems[s], v)
                hw[s] = v
            ins = op.fn(eng)
            if op.chan is not None:
                ins.then_inc(sems[("c", op.chan)], 16)
            elif op.needed:
                ins.then_inc(sems[("e", eng_name)], 1)
        for s in final_waits:
            eng.wait_ge(sems[s], self.chan_final[s[1]])


def _fm_unit(W, cols):
    sub = W[:, cols]
    return sub.reshape(8, 128, len(cols)).transpose(1, 0, 2).reshape(128, -1)


def _swap_cols(c0, hd, n):
    cols = np.arange(c0, c0 + n)
    r = (cols - c0) % hd
    half = hd // 2
    return np.where(r < half, cols + half, cols - half)


class Packer:
    def __init__(self):
        self.parts = []
        self.loads = {}
        self.off = 0

    def add(self, name, arr):
        arr = np.ascontiguousarray(arr, dtype=np.float32)
        assert arr.shape[0] == 128
        e = arr.shape[1]
        assert e % 16 == 0 and e <= SLOT, (name, e)
        self.loads[name] = (self.off, e)
        self.parts.append(arr.reshape(-1))
        self.off += arr.size

    def flat(self):
        return np.concatenate(self.parts)


def pack_weights(inp):
    pk = Packer()
    w_in = inp["w_in"][0]
    def pack_ffn(f):
        wg, wu, wd = inp[f"ffn{f}_w_gate"][0], inp[f"ffn{f}_w_up"][0], inp[f"ffn{f}_w_down"][0]
        for i in range(NFC // 2):
            us = []
            for fc in (2 * i, 2 * i + 1):
                cols = np.arange(fc * 128, fc * 128 + 128)
                us += [_fm_unit(wg, cols), _fm_unit(wu, cols)]
            pk.add(f"f{f}gu{i}", np.concatenate(us, axis=1))
        for oc in range(8):
            u = wd[:, oc * 128:(oc + 1) * 128].reshape(NFC, 128, 128).transpose(1, 0, 2).reshape(128, -1)
            pk.add(f"f{f}d{oc}", u)

    pack_ffn(1)
    rope_chunks = []
    for j in range(4):
        rope_chunks.append((O_AQ + 128 * j, 64, False))
    for j in range(4):
        rope_chunks.append((O_IQ + 128 * j, 64, False))
    rope_chunks.append((O_AK, 64, True))
    rope_chunks.append((O_IK, 64, True))
    for j in range(4):
        rope_chunks.append((O_RQ + 128 * j, 128, False))
    for j in range(4):
        rope_chunks.append((O_RK + 128 * j, 128, False))
    units = []
    for c0, hd, dup in rope_chunks:
        if dup:
            cols = np.concatenate([np.arange(c0, c0 + 64), np.arange(c0, c0 + 64)])
            sw = _swap_cols(c0, 64, 64)
            scols = np.concatenate([sw, sw])
        else:
            cols = np.arange(c0, c0 + 128)
            scols = _swap_cols(c0, hd, 128)
        units.append(np.concatenate([_fm_unit(w_in, cols), _fm_unit(w_in, scols)], axis=1))
    for i in range(9):
        pk.add(f"rope{i}", np.concatenate(units[2 * i:2 * i + 2], axis=1))
    avw = np.concatenate([np.arange(O_AV, O_AV + 64), np.arange(O_IW, O_IW + 8), np.arange(O_IW, O_IW + 8)])
    pk.add("avw", _fm_unit(w_in, avw))
    for i in range(2):
        pk.add(f"rv{i}", _fm_unit(w_in, np.arange(O_RV + 512 * i, O_RV + 512 * i + 512)))
    for i in range(2):
        pk.add(f"rg{i}", _fm_unit(w_in, np.arange(O_RG + 512 * i, O_RG + 512 * i + 512)))
    wa, wb, wo = inp["w_branch_a"][0], inp["w_branch_b"][0], inp["w_out"][0]
    for oc in range(8):
        cols = np.arange(oc * 128, oc * 128 + 128)
        ua = np.zeros((128, 8, 128), np.float32)
        for hh in range(8):
            hp, j2 = hh // 4, hh % 4
            h = 2 * j2 + hp
            ua[0:64, hh, :] = wa[h * 64:(h + 1) * 64, oc * 128:(oc + 1) * 128]
        pk.add(f"mg{oc}", np.concatenate([
            _fm_unit(w_in, O_GA + cols), _fm_unit(wb, cols), _fm_unit(w_in, O_GB + cols), ua.reshape(128, -1)], axis=1))
    for i in range(2):
        pk.add(f"wo{i}", np.concatenate([_fm_unit(wo, np.arange(oc * 128, oc * 128 + 128)) for oc in range(4 * i, 4 * i + 4)], axis=1))
    pack_ffn(2)
    wpg, wpp = inp["w_ple_gate"][0], inp["w_ple_proj"][0]
    for i in range(2):
        pk.add(f"pg{i}", np.concatenate([_fm_unit(wpg, np.arange(oc * 128, oc * 128 + 128)) for oc in range(4 * i, 4 * i + 4)], axis=1))
    upp = wpp.reshape(2, 128, 1024).transpose(1, 0, 2).reshape(128, -1)
    pk.add("pp", upp)
    return pk


def make_consts(inp):
    c = {}
    c["ident"] = np.eye(128, dtype=np.float32)
    c["sel65"] = np.zeros((128, 128), np.float32)
    c["sel65"][64, :] = 1.0
    h = np.arange(4, dtype=np.float64)
    gam = 1.0 - np.exp2(-5.0 - h)
    lg = np.log(gam)
    p = np.arange(128)[:, None].astype(np.float64)
    dec = np.zeros((128, 4, 640), np.float64)
    sc = 128.0 ** -0.5
    cc = np.arange(128)[None, :].astype(np.float64)
    vis = ((p < 64) | (cc >= 64)).astype(np.float64)
    m = np.arange(512)[None, :].astype(np.float64)
    for hh in range(4):
        dec[:, hh, 0:128] = sc * np.exp(lg[hh] * np.abs(cc - p)) * vis
        dec[:, hh, 128:640] = sc * np.exp(lg[hh] * (m - p))
    c["decay"] = dec.reshape(128, -1).astype(np.float32)
    c["gam"] = gam
    gains = np.zeros((128, 40), np.float32)
    for i, k in enumerate(["ffn1_norm", "mix_norm", "ffn2_norm", "ple_norm"]):
        gains[:, i * 8:(i + 1) * 8] = inp[k][0].reshape(8, 128).T
    gains[:, 32:40] = inp["final_norm"].reshape(8, 128).T
    c["gains"] = gains
    sm = np.zeros((128, 8), np.float32)
    pp = np.arange(128)
    inv64 = (np.float32(10000.0) ** (-(np.arange(32, dtype=np.float32) / np.float32(32)))).astype(np.float32)
    inv128 = (np.float32(10000.0) ** (-(np.arange(64, dtype=np.float32) / np.float32(64)))).astype(np.float32)
    sm[:, 0] = inv64[pp % 32]
    sm[:, 1] = inv128[pp % 64]
    sm[:, 2] = np.where((pp % 64) < 32, -1.0, 1.0)
    sm[:, 3] = np.where(pp < 64, -1.0, 1.0)
    sm[:, 4] = EPS
    sm[:, 5] = GN_EPS
    c["small"] = sm
    c["pow2"] = np.tile((2.0 ** -(np.arange(KBIS + 2, dtype=np.float64) + 1.0))[None, :], (128, 1)).astype(np.float32)
    c["gn_g"] = np.ascontiguousarray(inp["ret_gn"][0], dtype=np.float32)
    return c


def build_program(loads, nw, gam, n_groups=2 * NG_SEQ, debug=None, stop=None):
    nc = bass.Bass("TRN2", target_bir_lowering=False)
    P = Prog()
    dbg_out = {}

    xT = nc.dram_tensor("xT", [D, TOK_CORE], F32, kind="ExternalInput").ap()
    pT = nc.dram_tensor("pT", [256, TOK_CORE], F32, kind="ExternalInput").ap()
    posd = nc.dram_tensor("pos", [1, TOK_CORE], I32, kind="ExternalInput").ap()
    wflat = nc.dram_tensor("wflat", [nw], F32, kind="ExternalInput").ap()
    wbf = nc.dram_tensor("wbf", [nw], BF16, kind="Internal").ap()
    c_ident = nc.dram_tensor("c_ident", [128, 128], F32, kind="ExternalInput").ap()
    c_sel = nc.dram_tensor("c_sel65", [128, 128], F32, kind="ExternalInput").ap()
    c_decay = nc.dram_tensor("c_decay", [128, 2560], F32, kind="ExternalInput").ap()
    c_gains = nc.dram_tensor("c_gains", [128, 40], F32, kind="ExternalInput").ap()
    c_small = nc.dram_tensor("c_small", [128, 8], F32, kind="ExternalInput").ap()
    c_pow2 = nc.dram_tensor("c_pow2", [128, KBIS + 2], F32, kind="ExternalInput").ap()
    c_gng = nc.dram_tensor("c_gn_g", [1024], F32, kind="ExternalInput").ap()
    outT = nc.dram_tensor("outT", [D, TOK_CORE], F32, kind="ExternalOutput").ap()
    if debug:
        for nm in debug:
            dbg_out[nm] = nc.dram_tensor("dbg_" + nm, [128, 8 * T], F32, kind="ExternalOutput").ap()

    from contextlib import ExitStack
    es = ExitStack()

    def sb(name, cols, dt):
        return es.enter_context(nc.sbuf_tensor(name, [128, cols], dt))

    hT_t = sb("hT", 8 * T, F32)
    uT_t = sb("uT", 8 * T, BF16)
    R1 = sb("R1", 28672, U8)
    yrT_t = sb("yrT", 8 * T, BF16)
    akT_t = sb("akT", SEQ, BF16)
    ikT_t = sb("ikT", SEQ, BF16)
    avx_t = sb("avx", 16 * 66, BF16)
    rkT_t = sb("rkT", 4 * SEQ, BF16)
    rv_t = sb("rv", 16 * 1024, BF16)
    ropedec = sb("ropedec", 2560, F32)
    C64, S64, C128, S128 = [ropedec[:, i * T:(i + 1) * T] for i in range(4)]
    identf_t = sb("identf", 128, F32)
    ident_t = sb("ident", 128, BF16)
    ones_t = sb("ones", 128, BF16)
    sel_t = sb("sel65", 128, F32)
    decay_t = ropedec
    gng_t = sb("gn_g", 1024, F32)
    gains_t = sb("gains", 40, F32)
    small_t = sb("small", 8, F32)
    pow2_t = sb("pow2", KBIS + 2, F32)
    wring = sb("wring", NSLOT * SLOT, BF16)
    R2 = sb("R2", 45056, U8)
    rstd_t = R2[:, 40960:43008].bitcast(F32)
    sqr_t = R2[:, 43008:45056].bitcast(BF16)
    ps_t = es.enter_context(nc.psum_tensor("ps", [128, 4096], F32))

    hT = hT_t[:, :].rearrange("p (a t) -> p a t", a=8)
    uT = uT_t[:, :].rearrange("p (a t) -> p a t", a=8)
    yrT = yrT_t[:, :].rearrange("p (a t) -> p a t", a=8)
    aT = R1[:, 0:NFC * T * 2].bitcast(BF16).rearrange("p (a t) -> p a t", a=NFC)
    aqP = R1[:, 0:8192].bitcast(BF16).rearrange("p (h a t) -> p h a t", h=2, a=4)
    iqP = R1[:, 8192:16384].bitcast(BF16).rearrange("p (h a t) -> p h a t", h=2, a=4)
    rqT = R1[:, 16384:20480].bitcast(BF16).rearrange("p (a t) -> p a t", a=4)
    yaT = R1[:, 20480:28672].bitcast(BF16).rearrange("p (a t) -> p a t", a=8)
    avx = avx_t[:, :].rearrange("p (a t) -> p a t", a=16)
    rkT = rkT_t[:, :].rearrange("p (a t) -> p a t", a=4)
    rv = rv_t[:, :].rearrange("p (a t) -> p a t", a=16)
    decay = decay_t[:, :].rearrange("p (a t) -> p a t", a=4)

    def r2(off, nbytes, dt):
        return R2[:, off:off + nbytes].bitcast(dt)

    def bank(b, n=512, p0=0, p1=128):
        return ps_t[p0:p1, b * 512:b * 512 + n]

    def bank_bf(b, half, n=512, p0=0, p1=128):
        v = ps_t[p0:p1, (b + half) * 512:(b + half) * 512 + 256].bitcast(BF16)
        return v[:, 0:n]

    def dma(eng, out, in_, chan, after=()):
        return P.add(eng, lambda e, o=out, i=in_: e.dma_start(out=o, in_=i), outs=[out], ins=[in_], chan=chan, after=after)

    CH = 512 * 2048
    nch = (nw + CH - 1) // CH
    w2 = wflat.rearrange("(r c) -> r c", c=2048)
    wb2 = wbf.rearrange("(r c) -> r c", c=2048)
    cast_ops = []

    def emit_casts():
        for i in range(nch):
            r0 = i * 512
            r1_ = min(nw // 2048, r0 + 512)
            cast_ops.append(dma("pool", wb2[r0:r1_, :], w2[r0:r1_, :], ("cast", i),
                                after=cast_ops[i - 6:i - 5] if i >= 6 else ()))

    xstage = R2[:, 20480:20480 + 16384].bitcast(F32).rearrange("p (a t) -> p a t", a=8)

    def load_x(tok0):
        for kc in range(8):
            dma("pool", xstage[:, kc, :], xT[kc * 128:(kc + 1) * 128, tok0:tok0 + T], ("x", kc))

    cl = []
    cl.append(("identf", dma("sp", identf_t[:, :], c_ident[:, :], ("k", 0))))
    cl.append(("sel65", dma("sp", sel_t[:, :], c_sel[:, :], ("k", 1))))
    cl.append(("gains", dma("sp", gains_t[:, :], c_gains[:, :], ("k", 3))))
    cl.append(("small", dma("sp", small_t[:, :], c_small[:, :], ("k", 4))))
    cl.append(("pow2", dma("sp", pow2_t[:, :], c_pow2[:, :], ("k", 5))))
    cl.append(("gn_g", dma("sp", gng_t[:, :], c_gng.partition_broadcast(128), ("k", 6))))
    for nm, op in cl:
        if nm != "identf":
            P.mark_const(nm, op)
    o1 = P.add("dve", lambda e: e.tensor_copy(out=ident_t[:, :], in_=identf_t[:, :]), outs=[ident_t[:, :]], ins=[identf_t[:, :]])
    o2 = P.add("dve", lambda e: e.memset(ones_t[:, :], 1.0), outs=[ones_t[:, :]])
    o3 = P.add("dve", lambda e: e.memset(avx_t[:, :], 1.0), outs=[avx_t[:, :]])
    P.add("dve", lambda e: e.memset(R1[64:128, 20480:28672], 0), outs=[R1[64:128, 20480:28672]])
    P.mark_const("ident", o1)
    P.mark_const("ones", o2)

    ident = ident_t[:, :]
    ones = ones_t[:, :]

    def sm(col):
        return small_t[:, col:col + 1]

    ring_state = {"n": 0}

    def wload(name):
        off, e = loads[name]
        s = ring_state["n"] % NSLOT
        ring_state["n"] += 1
        dst = wring[:, s * SLOT:s * SLOT + e]
        src = wbf[off:off + 128 * e].rearrange("(p e) -> p e", p=128)
        dma("sp", dst, src, ("w", s))
        return dst

    def mm(out, lhsT, rhs, start, stop):
        P.add("pe", lambda e, o=out, l=lhsT, r=rhs, s=start, t=stop: e.matmul(o, lhsT=l, rhs=r, start=s, stop=t),
              outs=[out], ins=[lhsT, rhs])

    def tr(out, in_):
        P.add("pe", lambda e, o=out, i=in_: e.transpose(o, i, ident), outs=[out], ins=[in_])

    def act(out, in_, func, scale=None, bias=None, accum_out=None, extra_ins=()):
        kw = {}
        if scale is not None:
            kw["scale"] = scale
        if bias is not None:
            kw["bias"] = bias
        if accum_out is not None:
            kw["accum_out"] = accum_out
        outs = [out] + ([accum_out] if accum_out is not None else [])
        aps = [x for x in (scale, bias) if x is not None and not isinstance(x, (int, float))]
        P.add("act", lambda e, o=out, i=in_, f=func, k=kw: e.activation(out=o, in_=i, func=f, **k),
              outs=outs, ins=[in_] + aps + list(extra_ins))

    def tt(eng, out, in0, in1, op):
        P.add(eng, lambda e, o=out, a=in0, b=in1, p=op: e.tensor_tensor(out=o, in0=a, in1=b, op=p),
              outs=[out], ins=[in0, in1])

    def ts(eng, out, in0, s1, op0, s2=None, op1=None, accum_out=None):
        ins = [in0] + [x for x in (s1, s2) if not isinstance(x, (int, float, type(None)))]
        outs = [out] + ([accum_out] if accum_out is not None else [])
        kw = {}
        if op1 is not None:
            kw["op1"] = op1
        if accum_out is not None:
            kw["accum_out"] = accum_out
        P.add(eng, lambda e, o=out, a=in0, x=s1, y=s2, p=op0, k=kw: e.tensor_scalar(out=o, in0=a, scalar1=x, scalar2=y, op0=p, **k),
              outs=outs, ins=ins)

    def stt(out, in0, scalar, in1, op0, op1):
        ins = [in0, in1] + ([scalar] if not isinstance(scalar, (int, float)) else [])
        P.add("dve", lambda e, o=out, a=in0, s=scalar, b=in1, p=op0, q=op1: e.scalar_tensor_tensor(out=o, in0=a, scalar=s, in1=b, op0=p, op1=q),
              outs=[out], ins=ins)

    def cp(eng, out, in_):
        P.add(eng, lambda e, o=out, i=in_: e.tensor_copy(out=o, in_=i), outs=[out], ins=[in_])

    def memset(eng, out, val):
        P.add(eng, lambda e, o=out, v=val: e.memset(o, v), outs=[out])

    def dump(nm, ap3):
        if debug and nm in dbg_out:
            a = ap3.shape[1]
            dst = dbg_out[nm][:, 0:a * T].rearrange("p (a t) -> p a t", a=a)
            dma("pool", dst, ap3, ("dbg", nm))

    def rmsnorm_stats(src=None):
        src = hT if src is None else src
        ssq = bank(7)
        for kc in range(8):
            sq = sqr_t[:, (kc % 2) * T:(kc % 2 + 1) * T]
            act(sq, src[:, kc, :], AF.Square)
            mm(ssq, ones, sq, kc == 0, kc == 7)
        act(rstd_t[:, :], ssq, AF.Sqrt, scale=1.0 / D, bias=sm(4), extra_ins=[])
        P.add("dve", lambda e: e.reciprocal(out=rstd_t[:, :], in_=rstd_t[:, :]), outs=[rstd_t[:, :]], ins=[rstd_t[:, :]])

    def rmsnorm(gi, src=None, dst=None):
        rmsnorm_stats(src)
        src = hT if src is None else src
        dst = uT if dst is None else dst
        for kc in range(8):
            stt(dst[:, kc, :], src[:, kc, :], gains_t[:, gi * 8 + kc:gi * 8 + kc + 1], rstd_t[:, :], ALU.mult, ALU.mult)

    def ffn(f, gi, src=None, uin=None):
        for _ in ffn_gen(f, gi, src, uin):
            pass

    def ffn_gen(f, gi, src=None, uin=None):
        if uin is None:
            rmsnorm(gi, src)
            uin = uT
        src = hT if src is None else src
        yield
        sg_slots = [r2(0, 2048, F32), r2(2048, 2048, F32)]
        k = 0
        for i in range(NFC // 2):
            w = wload(f"f{f}gu{i}").rearrange("p (u kc j) -> p u kc j", u=4, kc=8)
            for q in range(2):
                fc = 2 * i + q
                gps, ups = bank(fc % 2), bank(2 + fc % 2)
                for kc in range(8):
                    mm(gps, w[:, 2 * q, kc, :], uin[:, kc, :], kc == 0, kc == 7)
                for kc in range(8):
                    mm(ups, w[:, 2 * q + 1, kc, :], uin[:, kc, :], kc == 0, kc == 7)
                sg = sg_slots[fc % 2]
                act(sg, gps, AF.Silu)
                tt("dve", aT[:, fc, :], ups, sg, ALU.mult)
                yield
        for oc in range(8):
            w = wload(f"f{f}d{oc}").rearrange("p (fc j) -> p fc j", fc=NFC)
            yps = bank(4 + oc % 2)
            for fc in range(NFC):
                mm(yps, w[:, fc, :], aT[:, fc, :], fc == 0, fc == NFC - 1)
            stt(hT[:, oc, :], yps, 0.5, src[:, oc, :], ALU.mult, ALU.add)
            yield

    def rope_tables(tok0):
        for _ in rope_tables_gen(tok0):
            pass

    def rope_tables_gen(tok0):
        posi = r2(20480, 2048, I32)
        posf = r2(22528, 2048, F32)
        ang = r2(24576, 2048, F32)
        tmp = r2(26624, 2048, F32)
        tmi = r2(28672, 2048, I32)
        red = r2(30720, 2048, F32)
        dma("pool", posi, posd[0, tok0:tok0 + T].partition_broadcast(128), ("pos", 0))
        cp("dve", posf, posi)
        C1 = 6.28125
        C2 = 2.0 * math.pi - 6.28125
        for col, (Ct, St) in enumerate(((C64, S64), (C128, S128))):
            ts("dve", ang, posf, sm(col), ALU.mult)
            for shift, dst, is_sin in ((0.0, St, True), (math.pi / 2, Ct, False)):
                ts("dve", tmp, ang, shift, ALU.add, 1.0 / (2.0 * math.pi), ALU.mult)
                cp("dve", tmi, tmp)
                cp("dve", tmp, tmi)
                stt(red, tmp, -C1, ang, ALU.mult, ALU.add)
                stt(red, tmp, -C2, red, ALU.mult, ALU.add)
                if shift != 0.0:
                    ts("dve", red, red, shift, ALU.add)
                yield
                ts("dve", tmp, red, math.pi, ALU.is_gt, -2.0 * math.pi, ALU.mult)
                tt("dve", red, red, tmp, ALU.add)
                ts("dve", tmp, red, -math.pi, ALU.is_lt, 2.0 * math.pi, ALU.mult)
                tt("dve", red, red, tmp, ALU.add)
                ts("dve", red, red, 3.1415925, ALU.min, -3.1415925, ALU.max)
                if is_sin:
                    act(dst[:, :], red, AF.Sin, scale=sm(2 + col))
                else:
                    act(dst[:, :], red, AF.Sin)
                yield

    import os
    PE2 = os.environ.get("PE2", "dve")
    PIPE = int(os.environ.get("PIPE", "1"))
    ACT_SHARE = float(os.environ.get("ACT_SHARE", "0.6"))

    def mixer_proj(g):
        gs = g % NG_SEQ
        tok_s = gs * T
        for qp in (aqP, iqP):
            memset("dve", qp[64:128, 0, :, :], 0.0)
            memset("dve", qp[0:64, 1, :, :], 0.0)
        dests = ([("pad", aqP, j) for j in range(4)] + [("pad", iqP, j) for j in range(4)] +
                 [akT_t[:, tok_s:tok_s + T], ikT_t[:, tok_s:tok_s + T]] +
                 [rqT[:, j, :] for j in range(4)] + [rkT[:, j, tok_s:tok_s + T] for j in range(4)])
        tabs = [(C64, S64)] * 10 + [(C128, S128)] * 8
        t1s = [r2(4096, 2048, F32), r2(6144, 2048, F32)]
        t2s = [r2(8192, 2048, F32), r2(10240, 2048, F32)]
        for i in range(9):
            w = wload(f"rope{i}").rearrange("p (u kc j) -> p u kc j", u=4, kc=8)
            for q in range(2):
                ci = 2 * i + q
                aps, bps = bank(ci % 2), bank(2 + ci % 2)
                for kc in range(8):
                    mm(aps, w[:, 2 * q, kc, :], uT[:, kc, :], kc == 0, kc == 7)
                for kc in range(8):
                    mm(bps, w[:, 2 * q + 1, kc, :], uT[:, kc, :], kc == 0, kc == 7)
                Ct, St = tabs[ci]
                t1, t2 = t1s[ci % 2], t2s[ci % 2]
                tt("dve", t1, aps, Ct[:, :], ALU.mult)
                tt("dve", t2, bps, St[:, :], ALU.mult)
                if isinstance(dests[ci], tuple):
                    _, qp, j = dests[ci]
                    tt("dve", qp[0:64, 0, j, :], t1[0:64, :], t2[0:64, :], ALU.add)
                    tt("dve", qp[64:128, 1, j, :], t1[64:128, :], t2[64:128, :], ALU.add)
                else:
                    tt(PE2, dests[ci], t1, t2, ALU.add)
        w = wload("avw").rearrange("p (kc j) -> p kc j", kc=8)
        dg = r2(33792, 4 * 8 * 4, F32).rearrange("p (a t) -> p a t", a=4)
        for jt in range(4):
            tb = gs * 4 + jt
            pso = bank(4 + jt % 2, 80)
            for kc in range(8):
                mm(pso, uT[:, kc, jt * 128:(jt + 1) * 128], w[:, kc, :], kc == 0, kc == 7)
            act(avx[:, tb, 0:64], pso[:, 0:64], AF.Copy)
            ts("dve", dg[:, jt, :], pso[:, 64:72], (8.0 ** -0.5) * (64.0 ** -0.5), ALU.mult)
        for i in range(2):
            w = wload(f"rv{i}").rearrange("p (kc j) -> p kc j", kc=8)
            for jt in range(4):
                tb = gs * 4 + jt
                pso = bank(4 + jt % 2)
                for kc in range(8):
                    mm(pso, uT[:, kc, jt * 128:(jt + 1) * 128], w[:, kc, :], kc == 0, kc == 7)
                act(rv[:, tb, i * 512:(i + 1) * 512], pso, AF.Copy)
        return dg

    def attn_bufs():
        b = {}
        b["score"] = [r2(0, 8192, F32), r2(35840, 8192, F32)]
        b["mask"] = r2(8192, 4096, BF16)
        b["maskT"] = r2(12288, 4096, BF16).rearrange("p (a t) -> p a t", a=16)
        b["rels"] = [r2(16384 + 1024 * i, 1024, BF16) for i in range(3)]
        b["Es"] = [r2(19456 + 1024 * i, 1024, BF16) for i in range(3)]
        b["Ps"] = [r2(22528 + 1024 * i, 1024, BF16) for i in range(3)]
        b["oT"] = r2(25600, 4096, F32).rearrange("p (a t) -> p a t", a=2)
        b["rcp"] = r2(29696, 2048, F32)
        b["diag"] = r2(31744, 2048, BF16).rearrange("p (a t) -> p a t", a=8)
        b["st"] = r2(34304, 512, F32)
        return b

    def attn_A(g, jt, dg, B_):
        gs = g % NG_SEQ
        t = gs * 4 + jt
        nk = 128 * (t + 1)
        nchk = (nk + 511) // 512
        qc = slice(jt * 128, (jt + 1) * 128)
        score_sb = B_["score"][jt % 2]
        rels, diag = B_["rels"], B_["diag"]
        for h in range(8):
            ts("dve", diag[:, h, :], ident, dg[:, jt, h:h + 1], ALU.mult)
        steps = [(h, c) for h in range(8) for c in range(nchk)]

        def cw(c):
            return min(512, nk - 512 * c)

        def mm1(i):
            h, c = steps[i]
            hp, j2 = h % 2, h // 2
            n = cw(c)
            mm(bank(4 + i % 2, n), iqP[:, hp, j2, qc], ikT_t[:, 512 * c:512 * c + n], True, True)
            if i % 3 != 2:
                act(rels[i % 3][:, 0:n], bank(4 + i % 2, n), AF.Relu)
            else:
                ts("dve", rels[i % 3][:, 0:n], bank(4 + i % 2, n), 0.0, ALU.max)

        def mm2(i):
            h, c = steps[i]
            n = cw(c)
            mm(bank(c, n), diag[:, h, :], rels[i % 3][:, 0:n], h == 0, h == 7)

        for i in range(len(steps) + 1):
            if i < len(steps):
                mm1(i)
            if i >= 1:
                mm2(i - 1)
            yield
        for c in range(nchk):
            n = cw(c)
            act(score_sb[:, 512 * c:512 * c + n], bank(c, n), AF.Copy)
        memset(PE2, score_sb[0:64, nk - 64:nk], NEG)
        yield

    def attn_B(g, jt, B_):
        gs = g % NG_SEQ
        t = gs * 4 + jt
        nk = 128 * (t + 1)
        score_sb = B_["score"][jt % 2]
        mask, maskT, st = B_["mask"], B_["maskT"], B_["st"]
        rmax, lo0, w0, mid, ssum, av_, thr, cntd, vv = [st[:, i:i + 1] for i in range(9)]
        wk = st[:, 12:12 + KBIS + 2]
        if t >= 2:
            P.add("dve", lambda e: e.tensor_reduce(out=rmax, in_=score_sb[:, 0:nk], axis=AX.X, op=ALU.max),
                  outs=[rmax], ins=[score_sb[:, 0:nk]])
            P.add("dve", lambda e: e.tensor_reduce(out=lo0, in_=score_sb[:, 0:256], axis=AX.X, op=ALU.min),
                  outs=[lo0], ins=[score_sb[:, 0:256]])
            tt("dve", w0, rmax, lo0, ALU.subtract)
            ts("dve", wk, pow2_t[:, :], w0, ALU.mult)
            tt("dve", mid, lo0, wk[:, 0:1], ALU.add)
            yield
            nA = int(round(nk * ACT_SHARE / 64.0)) * 64
            nA = max(64, min(nk - 64, nA))
            cthr = 255.75 - nA / 2.0
            for k in range(KBIS):
                act(mask[:, 0:nA], score_sb[:, 0:nA], AF.Sign, scale=-1.0, bias=mid, accum_out=ssum)
                ts("dve", mask[:, nA:nk], score_sb[:, nA:nk], mid, ALU.is_ge, None, ALU.add, accum_out=cntd)
                yield
                stt(vv, ssum, -0.5, cntd, ALU.mult, ALU.add)
                stt(av_, vv, cthr, wk[:, k:k + 1], ALU.is_ge, ALU.mult)
                stt(mid, mid, wk[:, k + 1:k + 2], av_, ALU.subtract, ALU.add)
                yield
            tt("dve", thr, mid, wk[:, KBIS:KBIS + 1], ALU.subtract)
        else:
            memset("dve", thr, -1.0e29)
        ts("dve", mask[:, 0:nk], score_sb[:, 0:nk], thr, ALU.is_ge)
        yield
        for kb0 in range(0, t + 1, 4):
            nb = min(4, t + 1 - kb0)
            half = (kb0 // 4) % 2
            for q in range(nb):
                kb = kb0 + q
                tr(bank_bf(6, half, 512)[:, q * 128:(q + 1) * 128], mask[:, kb * 128:(kb + 1) * 128])
            act(maskT[:, kb0:kb0 + nb, :], bank_bf(6, half, nb * 128).rearrange("p (a t) -> p a t", a=nb), AF.Copy)
            yield

    def attn_C(g, jt, B_):
        gs = g % NG_SEQ
        t = gs * 4 + jt
        qc = slice(jt * 128, (jt + 1) * 128)
        maskT, Es, Ps, oT_sb, rcp = B_["maskT"], B_["Es"], B_["Ps"], B_["oT"], B_["rcp"]
        asteps = [(kb, hp) for kb in range(t + 1) for hp in range(2)]

        def L(i):
            kb, hp = asteps[i]
            lps = bank(i % 3)
            mm(lps, akT_t[:, kb * 128:(kb + 1) * 128], aqP[:, hp, :, qc], True, True)
            act(Es[i % 3], lps, AF.Exp, scale=0.125)
            e3 = Es[i % 3].rearrange("p (a t) -> p a t", a=4)
            p3 = Ps[i % 3].rearrange("p (a t) -> p a t", a=4)
            m3 = maskT[:, kb:kb + 1, :].broadcast_to([128, 4, 128])
            tt(PE2, p3, e3, m3, ALU.mult)

        def V(i):
            kb, hp = asteps[i]
            mm(bank(4 + hp, 512, 0, 65), avx[:, kb, 0:65], Ps[i % 3], kb == 0, kb == t)

        na = len(asteps)
        for i in range(na + 2):
            if i < na:
                L(i)
            if i >= 2:
                V(i - 2)
            yield
        for hp in range(2):
            act(oT_sb[0:65, hp, :], bank(4 + hp, 512, 0, 65), AF.Copy)
        for hp in range(2):
            bcf = bank(3)
            bc = bank(3, 512, 0, 64)
            mm(bcf, sel_t[:, :], oT_sb[:, hp, :], True, True)
            P.add("dve", lambda e, b=bc: e.reciprocal(out=rcp[0:64, :], in_=b), outs=[rcp[0:64, :]], ins=[bc])
            tt("dve", yaT[0:64, hp * 4:hp * 4 + 4, qc],
               oT_sb[0:64, hp, :].rearrange("p (a t) -> p a t", a=4),
               rcp[0:64, :].rearrange("p (a t) -> p a t", a=4), ALU.mult)
        yield

    def run_interleaved(gens):
        live = []
        for gen, n in gens:
            live.append([gen, 0, max(1, n)])
        while live:
            live.sort(key=lambda x: x[1] / x[2])
            cur = live[0]
            try:
                next(cur[0])
                cur[1] += 1
            except StopIteration:
                live.remove(cur)

    def attention_group(g, dg):
        gs = g % NG_SEQ
        B_ = attn_bufs()
        memset("dve", r2(25600, 4096, F32), 0.0)

        def nA(jt):
            nk = 128 * (gs * 4 + jt + 1)
            return 8 * ((nk + 511) // 512) + 2

        def nB(jt):
            t = gs * 4 + jt
            return (2 * KBIS + 3 if t >= 2 else 1) + (t + 4) // 4

        def nC(jt):
            return 2 * (gs * 4 + jt + 1) + 3

        def mk(kind, jt):
            if kind == "A":
                return attn_A(g, jt, dg, B_), nA(jt)
            if kind == "B":
                return attn_B(g, jt, B_), nB(jt)
            return attn_C(g, jt, B_), nC(jt)

        def chain(items):
            def gen():
                for kind, jt in items:
                    yield from mk(kind, jt)[0]
            return gen(), sum(mk(kind, jt)[1] for kind, jt in items)

        if PIPE:
            stages = [[[("A", 0)]], [[("B", 0)], [("A", 1)]], [[("B", 1)], [("C", 0), ("A", 2)]],
                      [[("B", 2)], [("C", 1), ("A", 3)]], [[("B", 3)], [("C", 2)]], [[("C", 3)]]]
        else:
            stages = [[[(k, jt)]] for jt in range(4) for k in "ABC"]
        for stg in stages:
            run_interleaved([chain(items) for items in stg])

    def retention(g):
        gs = g % NG_SEQ
        sds = [r2(1024 * i, 1024, BF16) for i in range(3)]
        grg = r2(3072, 16384, F32).rearrange("p (a t) -> p a t", a=4)
        yr = r2(19456, 8192, BF16).rearrange("p (a t) -> p a t", a=4)
        yn = [r2(27648 + 1024 * i, 1024, F32) for i in range(4)]
        st = r2(31744, 512, F32)
        wr = [wload(f"rg{i}").rearrange("p (kc j) -> p kc j", kc=8) for i in range(2)]
        for jt in range(4):
            for i in range(2):
                pso = bank(4 + i)
                for kc in range(8):
                    mm(pso, uT[:, kc, jt * 128:(jt + 1) * 128], wr[i][:, kc, :], kc == 0, kc == 7)
                act(grg[:, jt, i * 512:(i + 1) * 512], pso, AF.Silu)
            tt(PE2, grg[:, jt, :], grg[:, jt, :], gng_t[:, :], ALU.mult)
        nkb = 4 * gs + 4
        k = 0
        for h in range(4):
            def S(kb, i):
                r = max(0, kb - 4 * gs)
                n = (4 - r) * 128
                sps = bank(4 + i % 3, n)
                mm(sps, rkT[:, h, kb * 128:(kb + 1) * 128], rqT[:, h, r * 128:512], True, True)
                sd = sds[i % 3]
                if kb < 4 * gs:
                    gm = float(gam[h] ** (128.0 * (4 * gs - kb)))
                    stt(sd[:, 0:512], sps, gm, decay[:, h, 128:640], ALU.mult, ALU.mult)
                else:
                    tt("dve", sd[:, 0:128], sps[:, 0:128], decay[:, h, 0:128], ALU.mult)
                    if n > 128:
                        tt("dve", sd[:, 128:n], sps[:, 128:n], decay[:, h, 256:256 + n - 128], ALU.mult)

            def PV(kb, i):
                r = max(0, kb - 4 * gs)
                sd = sds[i % 3]
                for jt in range(r, 4):
                    mm(bank(jt, 256), sd[:, (jt - r) * 128:(jt - r + 1) * 128], rv[:, kb, h * 256:(h + 1) * 256],
                       kb == 0, kb == 4 * gs + jt)

            for i in range(nkb + 2):
                if i < nkb:
                    S(i, i)
                if i >= 2:
                    PV(i - 2, i - 2)
            bsts = [st[:, jt * 8:jt * 8 + 6] for jt in range(4)]
            mvs = [st[:, 32 + jt * 2:32 + jt * 2 + 2] for jt in range(4)]
            rss = [st[:, 40 + jt:41 + jt] for jt in range(4)]
            for jt in range(4):
                P.add("dve", lambda e, o=bsts[jt], i=bank(jt, 256): e.bn_stats(out=o, in_=i), outs=[bsts[jt]], ins=[bank(jt, 256)])
            for jt in range(4):
                P.add("dve", lambda e, o=mvs[jt], i=bsts[jt]: e.bn_aggr(out=o, in_=i), outs=[mvs[jt]], ins=[bsts[jt]])
            for jt in range(4):
                act(rss[jt], mvs[jt][:, 1:2], AF.Sqrt, bias=sm(5))
            for jt in range(4):
                P.add("dve", lambda e, o=rss[jt]: e.reciprocal(out=o, in_=o), outs=[rss[jt]], ins=[rss[jt]])
            nmr = [st[:, 48 + jt:49 + jt] for jt in range(4)]
            for jt in range(4):
                stt(nmr[jt], mvs[jt][:, 0:1], -1.0, rss[jt], ALU.mult, ALU.mult)
            for jt in range(4):
                act(yn[jt][:, 0:256], bank(jt, 256), AF.Identity, scale=rss[jt], bias=nmr[jt])
            for jt in range(4):
                tt(PE2, yr[:, jt, h * 256:(h + 1) * 256], yn[jt][:, 0:256], grg[:, jt, h * 256:(h + 1) * 256], ALU.mult)
        for kc in range(8):
            half = kc % 2
            pst = bank_bf(6, half, 512)
            for jt in range(4):
                tr(pst[:, jt * 128:(jt + 1) * 128], yr[:, jt, kc * 128:(kc + 1) * 128])
            act(yrT[:, kc, :], pst, AF.Copy)

    def merge(g):
        m1s = [r2(0, 2048, F32), r2(2048, 2048, F32)]
        sgs = [r2(4096, 2048, F32), r2(6144, 2048, F32)]
        m2s = [r2(8192, 2048, F32), r2(10240, 2048, F32)]
        sg2 = [r2(12288, 2048, F32), r2(14336, 2048, F32)]
        mT = r2(16384, 8192, BF16).rearrange("p (a t) -> p a t", a=8)
        for oc in range(8):
            w = wload(f"mg{oc}").rearrange("p (u kc j) -> p u kc j", u=4, kc=8)
            q = oc % 2
            aps, gaps, bps, gbps = bank(q), bank(2 + q), bank(4 + q), bank(6 + q)
            for hh in range(8):
                mm(aps, w[:, 3, hh, :], yaT[:, hh, :], hh == 0, hh == 7)
            for kc in range(8):
                mm(gaps, w[:, 0, kc, :], uT[:, kc, :], kc == 0, kc == 7)
            for kc in range(8):
                mm(bps, w[:, 1, kc, :], yrT[:, kc, :], kc == 0, kc == 7)
            for kc in range(8):
                mm(gbps, w[:, 2, kc, :], uT[:, kc, :], kc == 0, kc == 7)
            act(sgs[q], gaps, AF.Sigmoid)
            act(sg2[q], gbps, AF.Sigmoid)
            tt("dve", m1s[q], aps, sgs[q], ALU.mult)
            tt("dve", m2s[q], bps, sg2[q], ALU.mult)
            tt(PE2, mT[:, oc, :], m1s[q], m2s[q], ALU.add)
        for i in range(2):
            w = wload(f"wo{i}").rearrange("p (u kc j) -> p u kc j", u=4, kc=8)
            for u in range(4):
                oc = 4 * i + u
                ops_ = bank(oc % 2)
                for kc in range(8):
                    mm(ops_, w[:, u, kc, :], mT[:, kc, :], kc == 0, kc == 7)
                tt("dve", hT[:, oc, :], ops_, hT[:, oc, :], ALU.add)

    def ple(g, tok0):
        pTf = r2(0, 4096, F32).rearrange("p (a t) -> p a t", a=2)
        pTb = r2(4096, 2048, BF16).rearrange("p (a t) -> p a t", a=2)
        pgs = [r2(8192, 2048, F32), r2(10240, 2048, F32)]
        ptm = [r2(12288, 2048, F32), r2(14336, 2048, F32)]
        dma("pool", pTf, pT[:, tok0:tok0 + T].rearrange("(a p) t -> p a t", p=128), ("p", 0))
        act(pTb, pTf, AF.Copy)
        rmsnorm(3)
        wpp = None
        for i in range(2):
            w = wload(f"pg{i}").rearrange("p (u kc j) -> p u kc j", u=4, kc=8)
            if wpp is None:
                wpp = wload("pp").rearrange("p (kc j) -> p kc j", kc=2)
            for u in range(4):
                oc = 4 * i + u
                q = oc % 2
                gps, pps = bank(q), bank(2 + q)
                for kc in range(8):
                    mm(gps, w[:, u, kc, :], uT[:, kc, :], kc == 0, kc == 7)
                for kc in range(2):
                    mm(pps, wpp[:, kc, oc * 128:(oc + 1) * 128], pTb[:, kc, :], kc == 0, kc == 1)
                act(pgs[q], gps, AF.Sigmoid)
                tt("dve", ptm[q], pps, pgs[q], ALU.mult)
                tt(PE2, hT[:, oc, :], hT[:, oc, :], ptm[q], ALU.add)

    def final(g, tok0):
        ost = [r2(16384, 2048, F32), r2(18432, 2048, F32)]
        rmsnorm_stats()
        for oc in range(8):
            o = ost[oc % 2]
            stt(o, hT[:, oc, :], gains_t[:, 32 + oc:33 + oc], rstd_t[:, :], ALU.mult, ALU.mult)
            dma("pool", outT[oc * 128:(oc + 1) * 128, tok0:tok0 + T], o, ("out", oc % 2))

    for g in range(n_groups):
        tok0 = g * T
        if g == 0:
            load_x(tok0)
            emit_casts()
        if stop == "x":
            final(g, tok0)
            continue
        if stop == "norm":
            rmsnorm(0)
            final(g, tok0)
            continue
        if g == 0:
            rmsnorm(0, xstage, yrT)
        ffn(1, 0, xstage, yrT)
        if g == 0:
            dump("h1", hT)
        if stop == "ffn1":
            final(g, tok0)
            continue
        if g == 0:
            rope_tables(tok0)
        if stop == "rope":
            final(g, tok0)
            continue
        rmsnorm(1)
        dg = mixer_proj(g)
        if stop == "proj":
            final(g, tok0)
            continue
        attention_group(g, dg)
        if stop == "attn":
            final(g, tok0)
            continue
        dma("sp", decay_t[:, :], c_decay[:, :], ("dec", 0))
        retention(g)
        if stop == "ret":
            final(g, tok0)
            continue
        merge(g)
        if g == 0:
            dump("h2", hT)
        if stop == "merge":
            final(g, tok0)
            continue
        if g + 1 < n_groups:
            run_interleaved([(ffn_gen(2, 2), 31), (rope_tables_gen(tok0 + T), 9)])
            load_x(tok0 + T)
        else:
            ffn(2, 2)
        ple(g, tok0)
        if g == 0:
            dump("h3", hT)
        if g + 1 < n_groups:
            rmsnorm(0, xstage, yrT)
        final(g, tok0)

    P.finalize()
    sems = {}
    for e in Prog.ENGS:
        sems[("e", e)] = es.enter_context(nc.semaphore("sem_" + e))
    for ch in P.chan_list:
        sems[("c", ch)] = es.enter_context(nc.semaphore("semc_" + "_".join(str(x) for x in ch)))
    finals = [("c", ch) for ch in P.chan_list if ch[0] in ("out", "dbg")]
    block = es.enter_context(nc.Block())

    @block.tensor
    def _(e):
        P.emit_stream("pe", e, sems)

    @block.scalar
    def _(e):
        P.emit_stream("act", e, sems)

    @block.vector
    def _(e):
        P.emit_stream("dve", e, sems)

    @block.gpsimd
    def _(e):
        P.emit_stream("pool", e, sems, final_waits=finals)

    @block.sync
    def _(e):
        P.emit_stream("sp", e, sems)

    es.close()
    return nc


def _prepare(inputs):
    inp = {k: np.asarray(v) for k, v in inputs.items()}
    pk = pack_weights(inp)
    wflat = pk.flat()
    pad = (-wflat.size) % (512 * 2048)
    if pad:
        wflat = np.concatenate([wflat, np.zeros(pad, np.float32)])
    consts = make_consts(inp)
    x = inp["x"].astype(np.float32, copy=False)
    p = inp["p"][0].astype(np.float32, copy=False)
    pos = inp["positions"].astype(np.int32, copy=False)
    in_maps = []
    for c in range(NCORES):
        b0 = c * SEQ_PER_CORE
        xc = np.ascontiguousarray(x[b0:b0 + SEQ_PER_CORE].reshape(TOK_CORE, D).T)
        pc = np.ascontiguousarray(p[b0:b0 + SEQ_PER_CORE].reshape(TOK_CORE, 256).T)
        posc = np.ascontiguousarray(pos[b0:b0 + SEQ_PER_CORE].reshape(1, TOK_CORE))
        in_maps.append({
            "xT": xc, "pT": pc, "pos": posc, "wflat": wflat,
            "c_ident": consts["ident"], "c_sel65": consts["sel65"], "c_decay": consts["decay"],
            "c_gains": consts["gains"], "c_small": consts["small"], "c_pow2": consts["pow2"],
            "c_gn_g": consts["gn_g"],
        })
    return pk, wflat, consts, in_maps


def kernel(**inputs):
    pk, wflat, consts, in_maps = _prepare(inputs)
    nc = build_program(pk.loads, wflat.size, consts["gam"])
    res = run_bass_kernel_spmd(nc, in_maps, core_ids=list(range(NCORES)))
    out = np.empty((BATCH, SEQ, D), np.float32)
    for c in range(NCORES):
        oT = np.asarray(res.results[c]["outT"])
        out[c * SEQ_PER_CORE:(c + 1) * SEQ_PER_CORE] = oT.T.reshape(SEQ_PER_CORE, SEQ, D)
    return out
```

```python
import math
import numpy as np
import concourse.bass as bass
import concourse.mybir as mybir
from concourse.bass_utils import run_bass_kernel_spmd

F32 = mybir.dt.float32
BF16 = mybir.dt.bfloat16
I32 = mybir.dt.int32
U8 = mybir.dt.uint8
AF = mybir.ActivationFunctionType
ALU = mybir.AluOpType
AX = mybir.AxisListType

D = 1024
DFF = 2816
NFC = DFF // 128
SEQ = 2048
BATCH = 16
NCORES = 8
T = 512
NG_SEQ = SEQ // T
SEQ_PER_CORE = BATCH // NCORES
TOK_CORE = SEQ_PER_CORE * SEQ
EPS = 1e-6
GN_EPS = 1e-5
KBIS = 20
NEG = -1.0e30
MASK_BIG = 2000.0
SLOT = 4096
NSLOT = 3

O_AQ, O_AK, O_AV, O_IQ, O_IK, O_IW = 0, 512, 576, 640, 1152, 1216
O_RQ, O_RK, O_RV, O_RG, O_GA, O_GB = 1224, 1736, 2248, 3272, 4296, 5320


class Op:
    __slots__ = ("eng", "fn", "deps", "needed", "event", "chan", "idx")

    def __init__(self, eng, fn, chan):
        self.eng = eng
        self.fn = fn
        self.deps = {}
        self.needed = False
        self.event = None
        self.chan = chan


class Prog:
    ENGS = ("pe", "act", "dve", "pool", "sp")

    def __init__(self):
        self.streams = {e: [] for e in self.ENGS}
        self.acc = {}
        self.const = {}
        self.chans = {}

    @staticmethod
    def rect(ap):
        t = ap.tensor
        name = t.name
        esz = mybir.dt.size(ap.dtype)
        dims = [(int(s), int(c)) for s, c in ap.ap]
        off = int(ap.offset)
        shp = [int(x) for x in t.shape]
        if str(ap.space) == "DRAM" or len(shp) == 1:
            ext = 1 + sum((c - 1) * abs(s) for s, c in dims)
            return name, (0, 1, off * esz, (off + ext) * esz)
        row = 1
        for x in shp[1:]:
            row *= x
        p0 = off // row
        c0 = off % row
        if dims[0][0] == row or dims[0][1] == 1:
            pc = dims[0][1]
            rest = dims[1:]
        elif dims[0][0] == 0:
            pc = 1
            rest = dims[1:]
        else:
            pc = 1
            rest = dims
        ext = 1 + sum((c - 1) * abs(s) for s, c in rest)
        if str(ap.space) == "PSUM":
            b0 = (c0 * esz) // 2048 * 2048
            b1 = ((c0 + ext) * esz + 2047) // 2048 * 2048
            return name, (p0 // 32 * 32, (p0 + pc + 31) // 32 * 32, b0, b1)
        return name, (p0, p0 + pc, c0 * esz, (c0 + ext) * esz)

    @staticmethod
    def ov(a, b):
        return a[0] < b[1] and b[0] < a[1] and a[2] < b[3] and b[2] < a[3]

    @staticmethod
    def covers(a, b):
        return a[0] <= b[0] and a[1] >= b[1] and a[2] <= b[2] and a[3] >= b[3]

    def add(self, eng, fn, outs=(), ins=(), chan=None, after=()):
        op = Op(eng, fn, chan)
        for a in after:
            op.deps[a] = "raw"
        rin = [self.rect(a) for a in ins]
        rout = [self.rect(a) for a in outs]
        for name, r in rin:
            if name in self.const:
                op.deps[self.const[name]] = "raw"
                continue
            for e in self.acc.get(name, ()):
                if e[2] and self.ov(e[0], r):
                    op.deps[e[1]] = "raw"
        for name, r in rout:
            for e in self.acc.get(name, ()):
                if self.ov(e[0], r):
                    if e[1] not in op.deps:
                        op.deps[e[1]] = "other"
        for name, r in rin:
            if name in self.const:
                continue
            lst = self.acc.setdefault(name, [])
            if chan is None:
                lst[:] = [e for e in lst if not ((not e[2]) and e[1].eng == eng and e[1].chan is None and e[0] == r)]
            lst.append((r, op, False))
        for name, r in rout:
            lst = self.acc.setdefault(name, [])
            lst[:] = [e for e in lst if not self.covers(r, e[0])]
            lst.append((r, op, True))
        op.deps.pop(op, None)
        self.streams[eng].append(op)
        return op

    def mark_const(self, name, op):
        self.const[name] = op
        self.acc.pop(name, None)

    def needs_wait(self, op, dep, kind):
        if dep.chan is not None:
            return True
        if op.chan is not None:
            return True
        if dep.eng != op.eng:
            return True
        if op.eng == "pe":
            return False
        return True

    def finalize(self):
        for e in self.ENGS:
            for op in self.streams[e]:
                for dep, kind in op.deps.items():
                    if self.needs_wait(op, dep, kind):
                        dep.needed = True
        cnt = {e: 0 for e in self.ENGS}
        ccnt = {}
        for e in self.ENGS:
            for op in self.streams[e]:
                if op.chan is not None:
                    ccnt[op.chan] = ccnt.get(op.chan, 0) + 16
                    op.event = (("c", op.chan), ccnt[op.chan])
                elif op.needed:
                    cnt[e] += 1
                    op.event = (("e", e), cnt[e])
        self.chan_list = sorted(ccnt.keys(), key=str)
        self.chan_final = ccnt

    def emit_stream(self, eng_name, eng, sems, final_waits=()):
        hw = {}
        for op in self.streams[eng_name]:
            waits = {}
            for dep, kind in op.deps.items():
                if not self.needs_wait(op, dep, kind):
                    continue
                s, v = dep.event
                if hw.get(s, 0) < v and waits.get(s, 0) < v:
                    waits[s] = v
            for s, v in waits.items():
                eng.wait_ge(sems[s], v)
                hw[s] = v
            ins = op.fn(eng)
            if op.chan is not None:
                ins.then_inc(sems[("c", op.chan)], 16)
            elif op.needed:
                ins.then_inc(sems[("e", eng_name)], 1)
        for s in final_waits:
            eng.wait_ge(sems[s], self.chan_final[s[1]])


def _fm_unit(W, cols):
    sub = W[:, cols]
    return sub.reshape(8, 128, len(cols)).transpose(1, 0, 2).reshape(128, -1)


def _swap_cols(c0, hd, n):
    cols = np.arange(c0, c0 + n)
    r = (cols - c0) % hd
    half = hd // 2
    return np.where(r < half, cols + half, cols - half)


class Packer:
    def __init__(self):
        self.parts = []
        self.loads = {}
        self.off = 0

    def add(self, name, arr):
        arr = np.ascontiguousarray(arr, dtype=np.float32)
        assert arr.shape[0] == 128
        e = arr.shape[1]
        assert e % 16 == 0 and e <= SLOT, (name, e)
        self.loads[name] = (self.off, e)
        self.parts.append(arr.reshape(-1))
        self.off += arr.size

    def flat(self):
        return np.concatenate(self.parts)


def pack_weights(inp):
    pk = Packer()
    w_in = inp["w_in"][0]
    def pack_ffn(f):
        wg, wu, wd = inp[f"ffn{f}_w_gate"][0], inp[f"ffn{f}_w_up"][0], inp[f"ffn{f}_w_down"][0]
        for i in range(NFC // 2):
            us = []
            for fc in (2 * i, 2 * i + 1):
                cols = np.arange(fc * 128, fc * 128 + 128)
                us += [_fm_unit(wg, cols), _fm_unit(wu, cols)]
            pk.add(f"f{f}gu{i}", np.concatenate(us, axis=1))
        for oc in range(8):
            u = wd[:, oc * 128:(oc + 1) * 128].reshape(NFC, 128, 128).transpose(1, 0, 2).reshape(128, -1)
            pk.add(f"f{f}d{oc}", u)

    pack_ffn(1)
    rope_chunks = []
    for j in range(4):
        rope_chunks.append((O_AQ + 128 * j, 64, False))
    for j in range(4):
        rope_chunks.append((O_IQ + 128 * j, 64, False))
    rope_chunks.append((O_AK, 64, True))
    rope_chunks.append((O_IK, 64, True))
    for j in range(4):
        rope_chunks.append((O_RQ + 128 * j, 128, False))
    for j in range(4):
        rope_chunks.append((O_RK + 128 * j, 128, False))
    units = []
    for c0, hd, dup in rope_chunks:
        if dup:
            cols = np.concatenate([np.arange(c0, c0 + 64), np.arange(c0, c0 + 64)])
            sw = _swap_cols(c0, 64, 64)
            scols = np.concatenate([sw, sw])
        else:
            cols = np.arange(c0, c0 + 128)
            scols = _swap_cols(c0, hd, 128)
        units.append(np.concatenate([_fm_unit(w_in, cols), _fm_unit(w_in, scols)], axis=1))
    for i in range(9):
        pk.add(f"rope{i}", np.concatenate(units[2 * i:2 * i + 2], axis=1))
    avw = np.concatenate([np.arange(O_AV, O_AV + 64), np.arange(O_IW, O_IW + 8), np.arange(O_IW, O_IW + 8)])
    pk.add("avw", _fm_unit(w_in, avw))
    for i in range(2):
        pk.add(f"rv{i}", _fm_unit(w_in, np.arange(O_RV + 512 * i, O_RV + 512 * i + 512)))
    for i in range(2):
        pk.add(f"rg{i}", _fm_unit(w_in, np.arange(O_RG + 512 * i, O_RG + 512 * i + 512)))
    wa, wb, wo = inp["w_branch_a"][0], inp["w_branch_b"][0], inp["w_out"][0]
    for oc in range(8):
        cols = np.arange(oc * 128, oc * 128 + 128)
        ua = np.zeros((128, 8, 128), np.float32)
        for hh in range(8):
            hp, j2 = hh // 4, hh % 4
            h = 2 * j2 + hp
            ua[0:64, hh, :] = wa[h * 64:(h + 1) * 64, oc * 128:(oc + 1) * 128]
        pk.add(f"mg{oc}", np.concatenate([
            _fm_unit(w_in, O_GA + cols), _fm_unit(wb, cols), _fm_unit(w_in, O_GB + cols), ua.reshape(128, -1)], axis=1))
    for i in range(2):
        pk.add(f"wo{i}", np.concatenate([_fm_unit(wo, np.arange(oc * 128, oc * 128 + 128)) for oc in range(4 * i, 4 * i + 4)], axis=1))
    pack_ffn(2)
    wpg, wpp = inp["w_ple_gate"][0], inp["w_ple_proj"][0]
    for i in range(2):
        pk.add(f"pg{i}", np.concatenate([_fm_unit(wpg, np.arange(oc * 128, oc * 128 + 128)) for oc in range(4 * i, 4 * i + 4)], axis=1))
    upp = wpp.reshape(2, 128, 1024).transpose(1, 0, 2).reshape(128, -1)
    pk.add("pp", upp)
    return pk


def make_consts(inp):
    c = {}
    c["ident"] = np.eye(128, dtype=np.float32)
    c["sel65"] = np.zeros((128, 128), np.float32)
    c["sel65"][64, :] = 1.0
    h = np.arange(4, dtype=np.float64)
    gam = 1.0 - np.exp2(-5.0 - h)
    lg = np.log(gam)
    p = np.arange(128)[:, None].astype(np.float64)
    dec = np.zeros((128, 4, 640), np.float64)
    sc = 128.0 ** -0.5
    cc = np.arange(128)[None, :].astype(np.float64)
    vis = ((p < 64) | (cc >= 64)).astype(np.float64)
    m = np.arange(512)[None, :].astype(np.float64)
    for hh in range(4):
        dec[:, hh, 0:128] = sc * np.exp(lg[hh] * np.abs(cc - p)) * vis
        dec[:, hh, 128:640] = sc * np.exp(lg[hh] * (m - p))
    c["decay"] = dec.reshape(128, -1).astype(np.float32)
    c["gam"] = gam
    gains = np.zeros((128, 40), np.float32)
    for i, k in enumerate(["ffn1_norm", "mix_norm", "ffn2_norm", "ple_norm"]):
        gains[:, i * 8:(i + 1) * 8] = inp[k][0].reshape(8, 128).T
    gains[:, 32:40] = inp["final_norm"].reshape(8, 128).T
    c["gains"] = gains
    sm = np.zeros((128, 8), np.float32)
    pp = np.arange(128)
    inv64 = (np.float32(10000.0) ** (-(np.arange(32, dtype=np.float32) / np.float32(32)))).astype(np.float32)
    inv128 = (np.float32(10000.0) ** (-(np.arange(64, dtype=np.float32) / np.float32(64)))).astype(np.float32)
    sm[:, 0] = inv64[pp % 32]
    sm[:, 1] = inv128[pp % 64]
    sm[:, 2] = np.where((pp % 64) < 32, -1.0, 1.0)
    sm[:, 3] = np.where(pp < 64, -1.0, 1.0)
    sm[:, 4] = EPS
    sm[:, 5] = GN_EPS
    sm[:, 6] = -MASK_BIG
    c["small"] = sm
    c["pow2"] = np.tile((2.0 ** -(np.arange(KBIS + 2, dtype=np.float64) + 1.0))[None, :], (128, 1)).astype(np.float32)
    c["gn_g"] = np.ascontiguousarray(inp["ret_gn"][0], dtype=np.float32)
    return c


def build_program(loads, nw, gam, n_groups=2 * NG_SEQ, debug=None, stop=None):
    nc = bass.Bass("TRN2", target_bir_lowering=False)
    P = Prog()
    dbg_out = {}

    xT = nc.dram_tensor("xT", [D, TOK_CORE], F32, kind="ExternalInput").ap()
    pT = nc.dram_tensor("pT", [256, TOK_CORE], F32, kind="ExternalInput").ap()
    posd = nc.dram_tensor("pos", [1, TOK_CORE], I32, kind="ExternalInput").ap()
    wflat = nc.dram_tensor("wflat", [nw], F32, kind="ExternalInput").ap()
    wbf = nc.dram_tensor("wbf", [nw], BF16, kind="Internal").ap()
    c_ident = nc.dram_tensor("c_ident", [128, 128], F32, kind="ExternalInput").ap()
    c_sel = nc.dram_tensor("c_sel65", [128, 128], F32, kind="ExternalInput").ap()
    c_decay = nc.dram_tensor("c_decay", [128, 2560], F32, kind="ExternalInput").ap()
    c_gains = nc.dram_tensor("c_gains", [128, 40], F32, kind="ExternalInput").ap()
    c_small = nc.dram_tensor("c_small", [128, 8], F32, kind="ExternalInput").ap()
    c_pow2 = nc.dram_tensor("c_pow2", [128, KBIS + 2], F32, kind="ExternalInput").ap()
    c_gng = nc.dram_tensor("c_gn_g", [1024], F32, kind="ExternalInput").ap()
    outT = nc.dram_tensor("outT", [D, TOK_CORE], F32, kind="ExternalOutput").ap()
    if debug:
        for nm in debug:
            dbg_out[nm] = nc.dram_tensor("dbg_" + nm, [128, 8 * T], F32, kind="ExternalOutput").ap()

    from contextlib import ExitStack
    es = ExitStack()

    def sb(name, cols, dt):
        return es.enter_context(nc.sbuf_tensor(name, [128, cols], dt))

    hT_t = sb("hT", 8 * T, F32)
    uT_t = sb("uT", 8 * T, BF16)
    R1 = sb("R1", 28672, U8)
    yrT_t = sb("yrT", 8 * T, BF16)
    akT_t = sb("akT", SEQ, BF16)
    ikT_t = sb("ikT", SEQ, BF16)
    avx_t = sb("avx", 16 * 66, BF16)
    rkT_t = sb("rkT", 4 * SEQ, BF16)
    rv_t = sb("rv", 16 * 1024, BF16)
    ropedec = sb("ropedec", 2560, F32)
    C64, S64, C128, S128 = [ropedec[:, i * T:(i + 1) * T] for i in range(4)]
    identf_t = sb("identf", 128, F32)
    ident_t = sb("ident", 128, BF16)
    ones_t = sb("ones", 128, BF16)
    sel_t = sb("sel65", 128, F32)
    decay_t = ropedec
    gng_t = sb("gn_g", 1024, F32)
    gains_t = sb("gains", 40, F32)
    small_t = sb("small", 8, F32)
    pow2_t = sb("pow2", KBIS + 2, F32)
    wring = sb("wring", NSLOT * SLOT, BF16)
    R2 = sb("R2", 49152, U8)
    rstd_t = R2[:, 40960:43008].bitcast(F32)
    sqr_t = R2[:, 43008:45056].bitcast(BF16)
    ps_t = es.enter_context(nc.psum_tensor("ps", [128, 4096], F32))

    hT = hT_t[:, :].rearrange("p (a t) -> p a t", a=8)
    uT = uT_t[:, :].rearrange("p (a t) -> p a t", a=8)
    yrT = yrT_t[:, :].rearrange("p (a t) -> p a t", a=8)
    aT = R1[:, 0:NFC * T * 2].bitcast(BF16).rearrange("p (a t) -> p a t", a=NFC)
    aqP = R1[:, 0:8192].bitcast(BF16).rearrange("p (h a t) -> p h a t", h=2, a=4)
    iqP = R1[:, 8192:16384].bitcast(BF16).rearrange("p (h a t) -> p h a t", h=2, a=4)
    rqT = R1[:, 16384:20480].bitcast(BF16).rearrange("p (a t) -> p a t", a=4)
    yaT = R1[:, 20480:28672].bitcast(BF16).rearrange("p (a t) -> p a t", a=8)
    avx = avx_t[:, :].rearrange("p (a t) -> p a t", a=16)
    rkT = rkT_t[:, :].rearrange("p (a t) -> p a t", a=4)
    rv = rv_t[:, :].rearrange("p (a t) -> p a t", a=16)
    decay = decay_t[:, :].rearrange("p (a t) -> p a t", a=4)

    def r2(off, nbytes, dt):
        return R2[:, off:off + nbytes].bitcast(dt)

    def bank(b, n=512, p0=0, p1=128):
        return ps_t[p0:p1, b * 512:b * 512 + n]

    def bank_bf(b, half, n=512, p0=0, p1=128):
        v = ps_t[p0:p1, (b + half) * 512:(b + half) * 512 + 256].bitcast(BF16)
        return v[:, 0:n]

    def dma(eng, out, in_, chan, after=()):
        return P.add(eng, lambda e, o=out, i=in_: e.dma_start(out=o, in_=i), outs=[out], ins=[in_], chan=chan, after=after)

    CH = 512 * 2048
    nch = (nw + CH - 1) // CH
    w2 = wflat.rearrange("(r c) -> r c", c=2048)
    wb2 = wbf.rearrange("(r c) -> r c", c=2048)
    cast_ops = []

    def emit_casts():
        for i in range(nch):
            r0 = i * 512
            r1_ = min(nw // 2048, r0 + 512)
            cast_ops.append(dma("pool", wb2[r0:r1_, :], w2[r0:r1_, :], ("cast", i),
                                after=cast_ops[i - 6:i - 5] if i >= 6 else ()))

    xstage = R2[:, 20480:20480 + 16384].bitcast(F32).rearrange("p (a t) -> p a t", a=8)

    def load_x(tok0):
        for kc in range(8):
            dma("pool", xstage[:, kc, :], xT[kc * 128:(kc + 1) * 128, tok0:tok0 + T], ("x", kc))

    cl = []
    cl.append(("identf", dma("sp", identf_t[:, :], c_ident[:, :], ("k", 0))))
    cl.append(("sel65", dma("sp", sel_t[:, :], c_sel[:, :], ("k", 1))))
    cl.append(("gains", dma("sp", gains_t[:, :], c_gains[:, :], ("k", 3))))
    cl.append(("small", dma("sp", small_t[:, :], c_small[:, :], ("k", 4))))
    cl.append(("pow2", dma("sp", pow2_t[:, :], c_pow2[:, :], ("k", 5))))
    cl.append(("gn_g", dma("sp", gng_t[:, :], c_gng.partition_broadcast(128), ("k", 6))))
    for nm, op in cl:
        if nm != "identf":
            P.mark_const(nm, op)
    o1 = P.add("dve", lambda e: e.tensor_copy(out=ident_t[:, :], in_=identf_t[:, :]), outs=[ident_t[:, :]], ins=[identf_t[:, :]])
    o2 = P.add("dve", lambda e: e.memset(ones_t[:, :], 1.0), outs=[ones_t[:, :]])
    o3 = P.add("dve", lambda e: e.memset(avx_t[:, :], 1.0), outs=[avx_t[:, :]])
    P.add("dve", lambda e: e.memset(R1[64:128, 20480:28672], 0), outs=[R1[64:128, 20480:28672]])
    P.mark_const("ident", o1)
    P.mark_const("ones", o2)

    ident = ident_t[:, :]
    ones = ones_t[:, :]

    def sm(col):
        return small_t[:, col:col + 1]

    ring_state = {"n": 0}

    def wload(name):
        off, e = loads[name]
        s = ring_state["n"] % NSLOT
        ring_state["n"] += 1
        dst = wring[:, s * SLOT:s * SLOT + e]
        src = wbf[off:off + 128 * e].rearrange("(p e) -> p e", p=128)
        dma("sp", dst, src, ("w", s))
        return dst

    def mm(out, lhsT, rhs, start, stop):
        P.add("pe", lambda e, o=out, l=lhsT, r=rhs, s=start, t=stop: e.matmul(o, lhsT=l, rhs=r, start=s, stop=t),
              outs=[out], ins=[lhsT, rhs])

    def tr(out, in_):
        P.add("pe", lambda e, o=out, i=in_: e.transpose(o, i, ident), outs=[out], ins=[in_])

    def act(out, in_, func, scale=None, bias=None, accum_out=None, extra_ins=()):
        kw = {}
        if scale is not None:
            kw["scale"] = scale
        if bias is not None:
            kw["bias"] = bias
        if accum_out is not None:
            kw["accum_out"] = accum_out
        outs = [out] + ([accum_out] if accum_out is not None else [])
        aps = [x for x in (scale, bias) if x is not None and not isinstance(x, (int, float))]
        P.add("act", lambda e, o=out, i=in_, f=func, k=kw: e.activation(out=o, in_=i, func=f, **k),
              outs=outs, ins=[in_] + aps + list(extra_ins))

    def tt(eng, out, in0, in1, op):
        P.add(eng, lambda e, o=out, a=in0, b=in1, p=op: e.tensor_tensor(out=o, in0=a, in1=b, op=p),
              outs=[out], ins=[in0, in1])

    def ts(eng, out, in0, s1, op0, s2=None, op1=None, accum_out=None):
        ins = [in0] + [x for x in (s1, s2) if not isinstance(x, (int, float, type(None)))]
        outs = [out] + ([accum_out] if accum_out is not None else [])
        kw = {}
        if op1 is not None:
            kw["op1"] = op1
        if accum_out is not None:
            kw["accum_out"] = accum_out
        P.add(eng, lambda e, o=out, a=in0, x=s1, y=s2, p=op0, k=kw: e.tensor_scalar(out=o, in0=a, scalar1=x, scalar2=y, op0=p, **k),
              outs=outs, ins=ins)

    def stt(out, in0, scalar, in1, op0, op1):
        ins = [in0, in1] + ([scalar] if not isinstance(scalar, (int, float)) else [])
        P.add("dve", lambda e, o=out, a=in0, s=scalar, b=in1, p=op0, q=op1: e.scalar_tensor_tensor(out=o, in0=a, scalar=s, in1=b, op0=p, op1=q),
              outs=[out], ins=ins)

    def cp(eng, out, in_):
        P.add(eng, lambda e, o=out, i=in_: e.tensor_copy(out=o, in_=i), outs=[out], ins=[in_])

    def memset(eng, out, val):
        P.add(eng, lambda e, o=out, v=val: e.memset(o, v), outs=[out])

    def dump(nm, ap3):
        if debug and nm in dbg_out:
            a = ap3.shape[1]
            dst = dbg_out[nm][:, 0:a * T].rearrange("p (a t) -> p a t", a=a)
            dma("pool", dst, ap3, ("dbg", nm))

    def rmsnorm_stats(src=None):
        src = hT if src is None else src
        ssq = bank(7)
        for kc in range(8):
            sq = sqr_t[:, (kc % 2) * T:(kc % 2 + 1) * T]
            act(sq, src[:, kc, :], AF.Square)
            mm(ssq, ones, sq, kc == 0, kc == 7)
        act(rstd_t[:, :], ssq, AF.Sqrt, scale=1.0 / D, bias=sm(4), extra_ins=[])
        P.add("dve", lambda e: e.reciprocal(out=rstd_t[:, :], in_=rstd_t[:, :]), outs=[rstd_t[:, :]], ins=[rstd_t[:, :]])

    def rmsnorm(gi, src=None, dst=None):
        rmsnorm_stats(src)
        src = hT if src is None else src
        dst = uT if dst is None else dst
        for kc in range(8):
            stt(dst[:, kc, :], src[:, kc, :], gains_t[:, gi * 8 + kc:gi * 8 + kc + 1], rstd_t[:, :], ALU.mult, ALU.mult)

    def ffn(f, gi, src=None, uin=None):
        for _ in ffn_gen(f, gi, src, uin):
            pass

    def ffn_gen(f, gi, src=None, uin=None):
        if uin is None:
            rmsnorm(gi, src)
            uin = uT
        src = hT if src is None else src
        yield
        sg_slots = [r2(0, 2048, F32), r2(2048, 2048, F32)]
        k = 0
        for i in range(NFC // 2):
            w = wload(f"f{f}gu{i}").rearrange("p (u kc j) -> p u kc j", u=4, kc=8)
            for q in range(2):
                fc = 2 * i + q
                gps, ups = bank(fc % 2), bank(2 + fc % 2)
                for kc in range(8):
                    mm(gps, w[:, 2 * q, kc, :], uin[:, kc, :], kc == 0, kc == 7)
                for kc in range(8):
                    mm(ups, w[:, 2 * q + 1, kc, :], uin[:, kc, :], kc == 0, kc == 7)
                sg = sg_slots[fc % 2]
                act(sg, gps, AF.Silu)
                tt("dve", aT[:, fc, :], ups, sg, ALU.mult)
                yield
        for oc in range(8):
            w = wload(f"f{f}d{oc}").rearrange("p (fc j) -> p fc j", fc=NFC)
            yps = bank(4 + oc % 2)
            for fc in range(NFC):
                mm(yps, w[:, fc, :], aT[:, fc, :], fc == 0, fc == NFC - 1)
            stt(hT[:, oc, :], yps, 0.5, src[:, oc, :], ALU.mult, ALU.add)
            yield

    def rope_tables(tok0):
        for _ in rope_tables_gen(tok0):
            pass

    def rope_tables_gen(tok0):
        posi = r2(20480, 2048, I32)
        posf = r2(22528, 2048, F32)
        ang = r2(24576, 2048, F32)
        tmp = r2(26624, 2048, F32)
        tmi = r2(28672, 2048, I32)
        red = r2(30720, 2048, F32)
        dma("pool", posi, posd[0, tok0:tok0 + T].partition_broadcast(128), ("pos", 0))
        cp("dve", posf, posi)
        C1 = 6.28125
        C2 = 2.0 * math.pi - 6.28125
        for col, (Ct, St) in enumerate(((C64, S64), (C128, S128))):
            ts("dve", ang, posf, sm(col), ALU.mult)
            for shift, dst, is_sin in ((0.0, St, True), (math.pi / 2, Ct, False)):
                ts("dve", tmp, ang, shift, ALU.add, 1.0 / (2.0 * math.pi), ALU.mult)
                cp("dve", tmi, tmp)
                cp("dve", tmp, tmi)
                stt(red, tmp, -C1, ang, ALU.mult, ALU.add)
                stt(red, tmp, -C2, red, ALU.mult, ALU.add)
                if shift != 0.0:
                    ts("dve", red, red, shift, ALU.add)
                yield
                ts("dve", tmp, red, math.pi, ALU.is_gt, -2.0 * math.pi, ALU.mult)
                tt("dve", red, red, tmp, ALU.add)
                ts("dve", tmp, red, -math.pi, ALU.is_lt, 2.0 * math.pi, ALU.mult)
                tt("dve", red, red, tmp, ALU.add)
                ts("dve", red, red, 3.1415925, ALU.min, -3.1415925, ALU.max)
                if is_sin:
                    act(dst[:, :], red, AF.Sin, scale=sm(2 + col))
                else:
                    act(dst[:, :], red, AF.Sin)
                yield

    import os
    PE2 = os.environ.get("PE2", "dve")
    PIPE = int(os.environ.get("PIPE", "1"))
    ACT_SHARE = float(os.environ.get("ACT_SHARE", "0.6"))
    COUNT_DVE = int(os.environ.get("COUNT_DVE", "1"))

    def mixer_proj(g):
        gs = g % NG_SEQ
        tok_s = gs * T
        for qp in (aqP, iqP):
            memset("dve", qp[64:128, 0, :, :], 0.0)
            memset("dve", qp[0:64, 1, :, :], 0.0)
        dests = ([("pad", aqP, j) for j in range(4)] + [("pad", iqP, j) for j in range(4)] +
                 [akT_t[:, tok_s:tok_s + T], ikT_t[:, tok_s:tok_s + T]] +
                 [rqT[:, j, :] for j in range(4)] + [rkT[:, j, tok_s:tok_s + T] for j in range(4)])
        tabs = [(C64, S64)] * 10 + [(C128, S128)] * 8
        t1s = [r2(4096, 2048, F32), r2(6144, 2048, F32)]
        t2s = [r2(8192, 2048, F32), r2(10240, 2048, F32)]
        for i in range(9):
            w = wload(f"rope{i}").rearrange("p (u kc j) -> p u kc j", u=4, kc=8)
            for q in range(2):
                ci = 2 * i + q
                aps, bps = bank(ci % 2), bank(2 + ci % 2)
                for kc in range(8):
                    mm(aps, w[:, 2 * q, kc, :], uT[:, kc, :], kc == 0, kc == 7)
                for kc in range(8):
                    mm(bps, w[:, 2 * q + 1, kc, :], uT[:, kc, :], kc == 0, kc == 7)
                Ct, St = tabs[ci]
                t1, t2 = t1s[ci % 2], t2s[ci % 2]
                tt("dve", t1, aps, Ct[:, :], ALU.mult)
                tt("dve", t2, bps, St[:, :], ALU.mult)
                if isinstance(dests[ci], tuple):
                    _, qp, j = dests[ci]
                    tt("dve", qp[0:64, 0, j, :], t1[0:64, :], t2[0:64, :], ALU.add)
                    tt("dve", qp[64:128, 1, j, :], t1[64:128, :], t2[64:128, :], ALU.add)
                else:
                    tt(PE2, dests[ci], t1, t2, ALU.add)
        w = wload("avw").rearrange("p (kc j) -> p kc j", kc=8)
        dg = r2(33792, 4 * 8 * 4, F32).rearrange("p (a t) -> p a t", a=4)
        for jt in range(4):
            tb = gs * 4 + jt
            pso = bank(4 + jt % 2, 80)
            for kc in range(8):
                mm(pso, uT[:, kc, jt * 128:(jt + 1) * 128], w[:, kc, :], kc == 0, kc == 7)
            act(avx[:, tb, 0:64], pso[:, 0:64], AF.Copy)
            ts("dve", dg[:, jt, :], pso[:, 64:72], (8.0 ** -0.5) * (64.0 ** -0.5), ALU.mult)
        for i in range(2):
            w = wload(f"rv{i}").rearrange("p (kc j) -> p kc j", kc=8)
            for jt in range(4):
                tb = gs * 4 + jt
                pso = bank(4 + jt % 2)
                for kc in range(8):
                    mm(pso, uT[:, kc, jt * 128:(jt + 1) * 128], w[:, kc, :], kc == 0, kc == 7)
                act(rv[:, tb, i * 512:(i + 1) * 512], pso, AF.Copy)
        return dg

    def attn_bufs():
        b = {}
        b["score"] = [r2(0, 8192, F32), r2(35840, 8192, F32)]
        b["mask"] = r2(8192, 4096, BF16)
        b["maskT"] = [r2(12288, 4096, BF16).rearrange("p (a t) -> p a t", a=16),
                      r2(45056, 4096, BF16).rearrange("p (a t) -> p a t", a=16)]
        b["rels"] = [r2(16384 + 1024 * i, 1024, BF16) for i in range(3)]
        b["Es"] = [r2(19456 + 1024 * i, 1024, BF16) for i in range(3)]
        b["Ps"] = [r2(22528 + 1024 * i, 1024, BF16) for i in range(3)]
        b["oT"] = r2(25600, 4096, F32).rearrange("p (a t) -> p a t", a=2)
        b["rcp"] = r2(29696, 2048, F32)
        b["diag"] = r2(31744, 2048, BF16).rearrange("p (a t) -> p a t", a=8)
        b["st"] = r2(34304, 512, F32)
        return b

    def attn_A(g, jt, dg, B_):
        gs = g % NG_SEQ
        t = gs * 4 + jt
        nk = 128 * (t + 1)
        nchk = (nk + 511) // 512
        qc = slice(jt * 128, (jt + 1) * 128)
        score_sb = B_["score"][jt % 2]
        rels, diag = B_["rels"], B_["diag"]
        for h in range(8):
            ts("dve", diag[:, h, :], ident, dg[:, jt, h:h + 1], ALU.mult)
        steps = [(h, c) for h in range(8) for c in range(nchk)]

        def cw(c):
            return min(512, nk - 512 * c)

        def mm1(i):
            h, c = steps[i]
            hp, j2 = h % 2, h // 2
            n = cw(c)
            mm(bank(4 + i % 2, n), iqP[:, hp, j2, qc], ikT_t[:, 512 * c:512 * c + n], True, True)
            if COUNT_DVE or i % 3 != 2:
                act(rels[i % 3][:, 0:n], bank(4 + i % 2, n), AF.Relu)
            else:
                ts("dve", rels[i % 3][:, 0:n], bank(4 + i % 2, n), 0.0, ALU.max)

        def mm2(i):
            h, c = steps[i]
            n = cw(c)
            mm(bank(c, n), diag[:, h, :], rels[i % 3][:, 0:n], h == 0, h == 7)

        for i in range(len(steps) + 1):
            if i < len(steps):
                mm1(i)
            if i >= 1:
                mm2(i - 1)
            yield
        for c in range(nchk):
            n = cw(c)
            act(score_sb[:, 512 * c:512 * c + n], bank(c, n), AF.Copy)
        memset(PE2, score_sb[0:64, nk - 64:nk], NEG)
        yield

    def attn_B(g, jt, B_):
        gs = g % NG_SEQ
        t = gs * 4 + jt
        nk = 128 * (t + 1)
        score_sb = B_["score"][jt % 2]
        mask, maskT, st = B_["mask"], B_["maskT"][jt % 2], B_["st"]
        rmax, lo0, w0, mid, ssum, av_, thr, cntd, vv = [st[:, i:i + 1] for i in range(9)]
        wk = st[:, 12:12 + KBIS + 2]
        if t >= 2:
            P.add("dve", lambda e: e.tensor_reduce(out=rmax, in_=score_sb[:, 0:nk], axis=AX.X, op=ALU.max),
                  outs=[rmax], ins=[score_sb[:, 0:nk]])
            P.add("dve", lambda e: e.tensor_reduce(out=lo0, in_=score_sb[:, 0:256], axis=AX.X, op=ALU.min),
                  outs=[lo0], ins=[score_sb[:, 0:256]])
            tt("dve", w0, rmax, lo0, ALU.subtract)
            ts("dve", wk, pow2_t[:, :], w0, ALU.mult)
            tt("dve", mid, lo0, wk[:, 0:1], ALU.add)
            yield
            nA = int(round(nk * ACT_SHARE / 64.0)) * 64
            nA = max(64, min(nk - 64, nA))
            cthr = 255.75 - nA / 2.0
            for k in range(KBIS):
                if COUNT_DVE:
                    ts("dve", mask[:, 0:nk], score_sb[:, 0:nk], mid, ALU.is_ge, None, ALU.add, accum_out=cntd)
                    stt(av_, cntd, 255.5, wk[:, k:k + 1], ALU.is_ge, ALU.mult)
                    stt(mid, mid, wk[:, k + 1:k + 2], av_, ALU.subtract, ALU.add)
                    yield
                    continue
                act(mask[:, 0:nA], score_sb[:, 0:nA], AF.Sign, scale=-1.0, bias=mid, accum_out=ssum)
                ts("dve", mask[:, nA:nk], score_sb[:, nA:nk], mid, ALU.is_ge, None, ALU.add, accum_out=cntd)
                yield
                stt(vv, ssum, -0.5, cntd, ALU.mult, ALU.add)
                stt(av_, vv, cthr, wk[:, k:k + 1], ALU.is_ge, ALU.mult)
                stt(mid, mid, wk[:, k + 1:k + 2], av_, ALU.subtract, ALU.add)
                yield
            tt("dve", thr, mid, wk[:, KBIS:KBIS + 1], ALU.subtract)
        else:
            memset("dve", thr, -1.0e29)
        ts("dve", mask[:, 0:nk], score_sb[:, 0:nk], thr, ALU.is_ge)
        yield
        for kb0 in range(0, t + 1, 4):
            nb = min(4, t + 1 - kb0)
            half = (kb0 // 4) % 2
            for q in range(nb):
                kb = kb0 + q
                tr(bank_bf(6, half, 512)[:, q * 128:(q + 1) * 128], mask[:, kb * 128:(kb + 1) * 128])
            act(maskT[:, kb0:kb0 + nb, :], bank_bf(6, half, nb * 128).rearrange("p (a t) -> p a t", a=nb), AF.Identity,
                scale=MASK_BIG, bias=sm(6))
            yield

    def attn_C(g, jt, B_):
        gs = g % NG_SEQ
        t = gs * 4 + jt
        qc = slice(jt * 128, (jt + 1) * 128)
        maskT, Es, Ps, oT_sb, rcp = B_["maskT"][jt % 2], B_["Es"], B_["Ps"], B_["oT"], B_["rcp"]
        asteps = [(kb, hp) for kb in range(t + 1) for hp in range(2)]

        def L(i):
            kb, hp = asteps[i]
            lps = bank(i % 3)
            mm(lps, akT_t[:, kb * 128:(kb + 1) * 128], aqP[:, hp, :, qc], True, False)
            for j2 in range(4):
                mm(lps[:, j2 * 128:(j2 + 1) * 128], ident, maskT[:, kb, :], False, j2 == 3)
            act(Ps[i % 3], lps, AF.Exp, scale=0.125)

        def V(i):
            kb, hp = asteps[i]
            mm(bank(4 + hp, 512, 0, 65), avx[:, kb, 0:65], Ps[i % 3], kb == 0, kb == t)

        na = len(asteps)
        for i in range(na + 2):
            if i < na:
                L(i)
            if i >= 2:
                V(i - 2)
            yield
        for hp in range(2):
            act(oT_sb[0:65, hp, :], bank(4 + hp, 512, 0, 65), AF.Copy)
        for hp in range(2):
            bcf = bank(3)
            bc = bank(3, 512, 0, 64)
            mm(bcf, sel_t[:, :], oT_sb[:, hp, :], True, True)
            P.add("dve", lambda e, b=bc: e.reciprocal(out=rcp[0:64, :], in_=b), outs=[rcp[0:64, :]], ins=[bc])
            tt("dve", yaT[0:64, hp * 4:hp * 4 + 4, qc],
               oT_sb[0:64, hp, :].rearrange("p (a t) -> p a t", a=4),
               rcp[0:64, :].rearrange("p (a t) -> p a t", a=4), ALU.mult)
        yield

    def run_interleaved(gens):
        live = []
        for gen, n in gens:
            live.append([gen, 0, max(1, n)])
        while live:
            live.sort(key=lambda x: x[1] / x[2])
            cur = live[0]
            try:
                next(cur[0])
                cur[1] += 1
            except StopIteration:
                live.remove(cur)

    def attention_group(g, dg):
        gs = g % NG_SEQ
        B_ = attn_bufs()
        memset("dve", r2(25600, 4096, F32), 0.0)

        def nA(jt):
            nk = 128 * (gs * 4 + jt + 1)
            return 8 * ((nk + 511) // 512) + 2

        def nB(jt):
            t = gs * 4 + jt
            return ((1 if COUNT_DVE else 2) * KBIS + 3 if t >= 2 else 1) + (t + 4) // 4

        def nC(jt):
            return 2 * (gs * 4 + jt + 1) + 3

        def mk(kind, jt):
            if kind == "A":
                return attn_A(g, jt, dg, B_), nA(jt)
            if kind == "B":
                return attn_B(g, jt, B_), nB(jt)
            return attn_C(g, jt, B_), nC(jt)

        def chain(items):
            def gen():
                for kind, jt in items:
                    yield from mk(kind, jt)[0]
            return gen(), sum(mk(kind, jt)[1] for kind, jt in items)

        if PIPE:
            stages = [[[("A", 0)]], [[("B", 0)], [("A", 1)]], [[("B", 1)], [("C", 0), ("A", 2)]],
                      [[("B", 2)], [("C", 1), ("A", 3)]], [[("B", 3)], [("C", 2)]], [[("C", 3)]]]
        else:
            stages = [[[(k, jt)]] for jt in range(4) for k in "ABC"]
        for stg in stages:
            run_interleaved([chain(items) for items in stg])

    def retention(g):
        gs = g % NG_SEQ
        sds = [r2(1024 * i, 1024, BF16) for i in range(3)]
        grg = r2(3072, 16384, F32).rearrange("p (a t) -> p a t", a=4)
        yr = r2(19456, 8192, BF16).rearrange("p (a t) -> p a t", a=4)
        yn = [r2(27648 + 1024 * i, 1024, F32) for i in range(4)]
        st = r2(31744, 512, F32)
        wr = [wload(f"rg{i}").rearrange("p (kc j) -> p kc j", kc=8) for i in range(2)]
        for jt in range(4):
            for i in range(2):
                pso = bank(4 + i)
                for kc in range(8):
                    mm(pso, uT[:, kc, jt * 128:(jt + 1) * 128], wr[i][:, kc, :], kc == 0, kc == 7)
                act(grg[:, jt, i * 512:(i + 1) * 512], pso, AF.Silu)
            tt(PE2, grg[:, jt, :], grg[:, jt, :], gng_t[:, :], ALU.mult)
        nkb = 4 * gs + 4
        k = 0
        for h in range(4):
            def S(kb, i):
                r = max(0, kb - 4 * gs)
                n = (4 - r) * 128
                sps = bank(4 + i % 3, n)
                mm(sps, rkT[:, h, kb * 128:(kb + 1) * 128], rqT[:, h, r * 128:512], True, True)
                sd = sds[i % 3]
                if kb < 4 * gs:
                    gm = float(gam[h] ** (128.0 * (4 * gs - kb)))
                    stt(sd[:, 0:512], sps, gm, decay[:, h, 128:640], ALU.mult, ALU.mult)
                else:
                    tt("dve", sd[:, 0:128], sps[:, 0:128], decay[:, h, 0:128], ALU.mult)
                    if n > 128:
                        tt("dve", sd[:, 128:n], sps[:, 128:n], decay[:, h, 256:256 + n - 128], ALU.mult)

            def PV(kb, i):
                r = max(0, kb - 4 * gs)
                sd = sds[i % 3]
                for jt in range(r, 4):
                    mm(bank(jt, 256), sd[:, (jt - r) * 128:(jt - r + 1) * 128], rv[:, kb, h * 256:(h + 1) * 256],
                       kb == 0, kb == 4 * gs + jt)

            for i in range(nkb + 2):
                if i < nkb:
                    S(i, i)
                if i >= 2:
                    PV(i - 2, i - 2)
            bsts = [st[:, jt * 8:jt * 8 + 6] for jt in range(4)]
            mvs = [st[:, 32 + jt * 2:32 + jt * 2 + 2] for jt in range(4)]
            rss = [st[:, 40 + jt:41 + jt] for jt in range(4)]
            for jt in range(4):
                P.add("dve", lambda e, o=bsts[jt], i=bank(jt, 256): e.bn_stats(out=o, in_=i), outs=[bsts[jt]], ins=[bank(jt, 256)])
            for jt in range(4):
                P.add("dve", lambda e, o=mvs[jt], i=bsts[jt]: e.bn_aggr(out=o, in_=i), outs=[mvs[jt]], ins=[bsts[jt]])
            for jt in range(4):
                act(rss[jt], mvs[jt][:, 1:2], AF.Sqrt, bias=sm(5))
            for jt in range(4):
                P.add("dve", lambda e, o=rss[jt]: e.reciprocal(out=o, in_=o), outs=[rss[jt]], ins=[rss[jt]])
            nmr = [st[:, 48 + jt:49 + jt] for jt in range(4)]
            for jt in range(4):
                stt(nmr[jt], mvs[jt][:, 0:1], -1.0, rss[jt], ALU.mult, ALU.mult)
            for jt in range(4):
                act(yn[jt][:, 0:256], bank(jt, 256), AF.Identity, scale=rss[jt], bias=nmr[jt])
            for jt in range(4):
                tt(PE2, yr[:, jt, h * 256:(h + 1) * 256], yn[jt][:, 0:256], grg[:, jt, h * 256:(h + 1) * 256], ALU.mult)
        for kc in range(8):
            half = kc % 2
            pst = bank_bf(6, half, 512)
            for jt in range(4):
                tr(pst[:, jt * 128:(jt + 1) * 128], yr[:, jt, kc * 128:(kc + 1) * 128])
            act(yrT[:, kc, :], pst, AF.Copy)

    def merge(g):
        m1s = [r2(0, 2048, F32), r2(2048, 2048, F32)]
        sgs = [r2(4096, 2048, F32), r2(6144, 2048, F32)]
        m2s = [r2(8192, 2048, F32), r2(10240, 2048, F32)]
        sg2 = [r2(12288, 2048, F32), r2(14336, 2048, F32)]
        mT = r2(16384, 8192, BF16).rearrange("p (a t) -> p a t", a=8)
        for oc in range(8):
            w = wload(f"mg{oc}").rearrange("p (u kc j) -> p u kc j", u=4, kc=8)
            q = oc % 2
            aps, gaps, bps, gbps = bank(q), bank(2 + q), bank(4 + q), bank(6 + q)
            for hh in range(8):
                mm(aps, w[:, 3, hh, :], yaT[:, hh, :], hh == 0, hh == 7)
            for kc in range(8):
                mm(gaps, w[:, 0, kc, :], uT[:, kc, :], kc == 0, kc == 7)
            for kc in range(8):
                mm(bps, w[:, 1, kc, :], yrT[:, kc, :], kc == 0, kc == 7)
            for kc in range(8):
                mm(gbps, w[:, 2, kc, :], uT[:, kc, :], kc == 0, kc == 7)
            act(sgs[q], gaps, AF.Sigmoid)
            act(sg2[q], gbps, AF.Sigmoid)
            tt("dve", m1s[q], aps, sgs[q], ALU.mult)
            tt("dve", m2s[q], bps, sg2[q], ALU.mult)
            tt(PE2, mT[:, oc, :], m1s[q], m2s[q], ALU.add)
        for i in range(2):
            w = wload(f"wo{i}").rearrange("p (u kc j) -> p u kc j", u=4, kc=8)
            for u in range(4):
                oc = 4 * i + u
                ops_ = bank(oc % 2)
                for kc in range(8):
                    mm(ops_, w[:, u, kc, :], mT[:, kc, :], kc == 0, kc == 7)
                tt("dve", hT[:, oc, :], ops_, hT[:, oc, :], ALU.add)

    def ple(g, tok0):
        pTf = r2(0, 4096, F32).rearrange("p (a t) -> p a t", a=2)
        pTb = r2(4096, 2048, BF16).rearrange("p (a t) -> p a t", a=2)
        pgs = [r2(8192, 2048, F32), r2(10240, 2048, F32)]
        ptm = [r2(12288, 2048, F32), r2(14336, 2048, F32)]
        dma("pool", pTf, pT[:, tok0:tok0 + T].rearrange("(a p) t -> p a t", p=128), ("p", 0))
        act(pTb, pTf, AF.Copy)
        rmsnorm(3)
        wpp = None
        for i in range(2):
            w = wload(f"pg{i}").rearrange("p (u kc j) -> p u kc j", u=4, kc=8)
            if wpp is None:
                wpp = wload("pp").rearrange("p (kc j) -> p kc j", kc=2)
            for u in range(4):
                oc = 4 * i + u
                q = oc % 2
                gps, pps = bank(q), bank(2 + q)
                for kc in range(8):
                    mm(gps, w[:, u, kc, :], uT[:, kc, :], kc == 0, kc == 7)
                for kc in range(2):
                    mm(pps, wpp[:, kc, oc * 128:(oc + 1) * 128], pTb[:, kc, :], kc == 0, kc == 1)
                act(pgs[q], gps, AF.Sigmoid)
                tt("dve", ptm[q], pps, pgs[q], ALU.mult)
                tt(PE2, hT[:, oc, :], hT[:, oc, :], ptm[q], ALU.add)

    def final(g, tok0):
        ost = [r2(16384, 2048, F32), r2(18432, 2048, F32)]
        rmsnorm_stats()
        for oc in range(8):
            o = ost[oc % 2]
            stt(o, hT[:, oc, :], gains_t[:, 32 + oc:33 + oc], rstd_t[:, :], ALU.mult, ALU.mult)
            dma("pool", outT[oc * 128:(oc + 1) * 128, tok0:tok0 + T], o, ("out", oc % 2))

    for g in range(n_groups):
        tok0 = g * T
        if g == 0:
            load_x(tok0)
            emit_casts()
        if stop == "x":
            final(g, tok0)
            continue
        if stop == "norm":
            rmsnorm(0)
            final(g, tok0)
            continue
        if g == 0:
            rmsnorm(0, xstage, yrT)
        ffn(1, 0, xstage, yrT)
        if g == 0:
            dump("h1", hT)
        if stop == "ffn1":
            final(g, tok0)
            continue
        if g == 0:
            rope_tables(tok0)
        if stop == "rope":
            final(g, tok0)
            continue
        rmsnorm(1)
        dg = mixer_proj(g)
        if stop == "proj":
            final(g, tok0)
            continue
        attention_group(g, dg)
        if stop == "attn":
            final(g, tok0)
            continue
        dma("sp", decay_t[:, :], c_decay[:, :], ("dec", 0))
        retention(g)
        if stop == "ret":
            final(g, tok0)
            continue
        merge(g)
        if g == 0:
            dump("h2", hT)
        if stop == "merge":
            final(g, tok0)
            continue
        if g + 1 < n_groups:
            run_interleaved([(ffn_gen(2, 2), 31), (rope_tables_gen(tok0 + T), 9)])
            load_x(tok0 + T)
        else:
            ffn(2, 2)
        ple(g, tok0)
        if g == 0:
            dump("h3", hT)
        if g + 1 < n_groups:
            rmsnorm(0, xstage, yrT)
        final(g, tok0)

    P.finalize()
    sems = {}
    for e in Prog.ENGS:
        sems[("e", e)] = es.enter_context(nc.semaphore("sem_" + e))
    for ch in P.chan_list:
        sems[("c", ch)] = es.enter_context(nc.semaphore("semc_" + "_".join(str(x) for x in ch)))
    finals = [("c", ch) for ch in P.chan_list if ch[0] in ("out", "dbg")]
    block = es.enter_context(nc.Block())

    @block.tensor
    def _(e):
        P.emit_stream("pe", e, sems)

    @block.scalar
    def _(e):
        P.emit_stream("act", e, sems)

    @block.vector
    def _(e):
        P.emit_stream("dve", e, sems)

    @block.gpsimd
    def _(e):
        P.emit_stream("pool", e, sems, final_waits=finals)

    @block.sync
    def _(e):
        P.emit_stream("sp", e, sems)

    es.close()
    return nc


def _prepare(inputs):
    inp = {k: np.asarray(v) for k, v in inputs.items()}
    pk = pack_weights(inp)
    wflat = pk.flat()
    pad = (-wflat.size) % (512 * 2048)
    if pad:
        wflat = np.concatenate([wflat, np.zeros(pad, np.float32)])
    consts = make_consts(inp)
    x = inp["x"].astype(np.float32, copy=False)
    p = inp["p"][0].astype(np.float32, copy=False)
    pos = inp["positions"].astype(np.int32, copy=False)
    in_maps = []
    for c in range(NCORES):
        b0 = c * SEQ_PER_CORE
        xc = np.ascontiguousarray(x[b0:b0 + SEQ_PER_CORE].reshape(TOK_CORE, D).T)
        pc = np.ascontiguousarray(p[b0:b0 + SEQ_PER_CORE].reshape(TOK_CORE, 256).T)
        posc = np.ascontiguousarray(pos[b0:b0 + SEQ_PER_CORE].reshape(1, TOK_CORE))
        in_maps.append({
            "xT": xc, "pT": pc, "pos": posc, "wflat": wflat,
            "c_ident": consts["ident"], "c_sel65": consts["sel65"], "c_decay": consts["decay"],
            "c_gains": consts["gains"], "c_small": consts["small"], "c_pow2": consts["pow2"],
            "c_gn_g": consts["gn_g"],
        })
    return pk, wflat, consts, in_maps


def kernel(**inputs):
    pk, wflat, consts, in_maps = _prepare(inputs)
    nc = build_program(pk.loads, wflat.size, consts["gam"])
    res = run_bass_kernel_spmd(nc, in_maps, core_ids=list(range(NCORES)))
    out = np.empty((BATCH, SEQ, D), np.float32)
    for c in range(NCORES):
        oT = np.asarray(res.results[c]["outT"])
        out[c * SEQ_PER_CORE:(c + 1) * SEQ_PER_CORE] = oT.T.reshape(SEQ_PER_CORE, SEQ, D)
    return out
```
